# Optimizing a Trainium2 kernel written in Bass

```python
import jax, jax.numpy as jnp
from jax import lax
import numpy as np

D_MODEL = 1024
BATCH = 32
SEQ = 256
DEPTH = 2
DEC_BATCH = 2
DEC_SEQ = 1024
PAST_LEN = 256

GRID_W = 64
HEAD_DIM = 64
A_HEADS = 4
A_KV_HEADS = 2
B_HEADS = 6
B_KV_HEADS = 2
C_HEADS = 6
A_DIM = A_HEADS * HEAD_DIM
B_DIM = B_HEADS * HEAD_DIM
C_DIM = C_HEADS * HEAD_DIM
MIX_DIM = A_DIM + B_DIM + C_DIM
WINDOW = 128
Q_BLK = 128
W_RANK = 64
A_RANK = 64
G_RANK = 128
FF_DIM = -(-8 * D_MODEL // (3 * 256)) * 256
ROPE_THETA = 10000.0
ROPE_PAIRS_AXIS = HEAD_DIM // 4
NORM_EPS = 1e-6
GN_EPS = 64e-5
NEG_INF = -1e30
IN_SPLITS = (A_DIM, A_KV_HEADS * HEAD_DIM, A_KV_HEADS * HEAD_DIM,
             B_DIM, B_KV_HEADS * HEAD_DIM, B_KV_HEADS * HEAD_DIM,
             C_DIM, C_DIM, C_DIM, W_RANK, A_RANK, G_RANK)
IN_COLS = sum(IN_SPLITS)

kernel_name = 'hybrid_diffusion_prefix_trunk_step'


def rms_norm(x, g):
    xf = x.astype(jnp.float32)
    y = xf * lax.rsqrt(jnp.mean(xf * xf, axis=-1, keepdims=True) + NORM_EPS)
    return y.astype(x.dtype) * g


def rope_2d(x):
    T = x.shape[1]
    rows = T // GRID_W
    f32 = jnp.float32
    row = jnp.repeat(jnp.arange(rows), GRID_W).astype(f32)
    col = (jnp.arange(T) % GRID_W).astype(f32)
    freqs = ROPE_THETA ** (-jnp.arange(ROPE_PAIRS_AXIS, dtype=f32) / ROPE_PAIRS_AXIS)
    ang = jnp.concatenate([row[:, None] * freqs, col[:, None] * freqs], axis=-1)[None, :, None, :]
    cos, sin = jnp.cos(ang), jnp.sin(ang)
    xf = x.astype(f32)
    half = HEAD_DIM // 2
    x1, x2 = xf[..., :half], xf[..., half:]
    return jnp.concatenate([x1 * cos - x2 * sin, x1 * sin + x2 * cos], axis=-1).astype(x.dtype)


def dense_attn(q, k, v, sink=None):
    B, T, Hq, D = q.shape
    Hkv = k.shape[2]
    G = Hq // Hkv
    nb = T // Q_BLK
    scale = D ** -0.5
    qb = jnp.moveaxis(q.reshape(B, nb, Q_BLK, Hkv, G, D), 1, 0)

    def one_block(qi):
        s = jnp.einsum('bqhgd,bshd->bhgqs', qi, k).astype(jnp.float32) * scale
        if sink is None:
            p = jax.nn.softmax(s, axis=-1)
        else:
            sl = jnp.broadcast_to(sink.astype(jnp.float32).reshape(Hkv, G)[None, :, :, None, None],
                                  s.shape[:-1] + (1,))
            p = jax.nn.softmax(jnp.concatenate([s, sl], axis=-1), axis=-1)[..., :-1]
        return jnp.einsum('bhgqs,bshd->bqhgd', p.astype(v.dtype), v)

    o = lax.map(one_block, qb)
    return jnp.moveaxis(o, 0, 1).reshape(B, T, Hq, D)


def window_attn(q, k, v, k_ctx, v_ctx, sink):
    B, T, Hq, D = q.shape
    Hkv = k.shape[2]
    G = Hq // Hkv
    nb = T // Q_BLK
    scale = D ** -0.5
    qb = q.reshape(B, nb, Q_BLK, Hkv, G, D)

    def band(t):
        tp = jnp.pad(t, ((0, 0), (Q_BLK, Q_BLK), (0, 0), (0, 0))).reshape(B, nb + 2, Q_BLK, Hkv, D)
        return jnp.concatenate([tp[:, :-2], tp[:, 1:-1], tp[:, 2:]], axis=2)

    kb, vb = band(k), band(v)
    blk = jnp.arange(nb)[:, None] * Q_BLK
    qpos = blk + jnp.arange(Q_BLK)[None, :]
    kpos = blk - Q_BLK + jnp.arange(3 * Q_BLK)[None, :]
    mask = ((jnp.abs(kpos[:, None, :] - qpos[:, :, None]) <= WINDOW)
            & (kpos[:, None, :] >= 0) & (kpos[:, None, :] < T))
    s_loc = jnp.einsum('bnqhgd,bnkhd->bnhgqk', qb, kb).astype(jnp.float32) * scale
    s_loc = jnp.where(mask[None, :, None, None], s_loc, NEG_INF)
    s_ctx = jnp.einsum('bnqhgd,bshd->bnhgqs', qb, k_ctx).astype(jnp.float32) * scale
    s_sink = jnp.broadcast_to(sink.astype(jnp.float32).reshape(Hkv, G)[None, None, :, :, None, None],
                              s_loc.shape[:-1] + (1,))
    p = jax.nn.softmax(jnp.concatenate([s_loc, s_ctx, s_sink], axis=-1), axis=-1)
    n_loc = 3 * Q_BLK
    n_ctx = k_ctx.shape[1]
    p_loc = p[..., :n_loc].astype(v.dtype)
    p_ctx = p[..., n_loc:n_loc + n_ctx].astype(v.dtype)
    o = (jnp.einsum('bnhgqk,bnkhd->bnqhgd', p_loc, vb)
         + jnp.einsum('bnhgqs,bshd->bnqhgd', p_ctx, v_ctx))
    return o.reshape(B, T, Hq, D)


def wkv_step(S, inp):
    r, w, k, v, a, b = inp
    sa = jnp.einsum('bhvk,bhk->bhv', S, a)
    S = S * w[:, :, None, :] + sa[..., None] * b[:, :, None, :] + v[..., None] * k[:, :, None, :]
    y = jnp.einsum('bhvk,bhk->bhv', S, r)
    return S, y


def rwkv_mix(r, k, v, xw, xa, xg, P, l, init_state):
    dtype = r.dtype
    f32 = jnp.float32
    B, T, _ = r.shape
    r, k, v, xw, xa, xg = (t.astype(f32) for t in (r, k, v, xw, xa, xg))

    def heads(t):
        return t.reshape(B, T, C_HEADS, HEAD_DIM)

    g = jax.nn.sigmoid(xg) @ P['c_g_up'][l].astype(f32)
    tw = jnp.tanh(xw)
    kk = heads(k * P['c_k_k'][l].astype(f32))
    kk = kk * lax.rsqrt(jnp.sum(kk * kk, axis=-1, keepdims=True) + 1e-12)
    rh, vh = heads(r), heads(v)
    r_k = P['c_r_k'][l].astype(f32)
    y_sum = jnp.zeros_like(rh)
    bonus = jnp.zeros_like(rh)
    finals = []
    for d in range(2):
        wlog = -jax.nn.softplus(-(P['c_w0'][l, d].astype(f32) + tw @ P['c_w_up'][l, d].astype(f32))) - 0.5
        decay = jnp.exp(-jnp.exp(wlog))
        a = jax.nn.sigmoid(P['c_a0'][l, d].astype(f32) + xa @ P['c_a_up'][l, d].astype(f32))
        kt = heads(k * (1.0 + (a - 1.0) * P['c_k_a'][l].astype(f32)))
        ah = heads(a)
        xs = tuple(jnp.moveaxis(t, 1, 0) for t in (rh, heads(decay), kt, vh, -kk, kk * ah))
        S_fin, y = lax.scan(wkv_step, init_state[:, d].astype(f32), xs, reverse=(d == 1))
        y_sum = y_sum + jnp.moveaxis(y, 0, 1)
        bonus = bonus + jnp.sum(rh * kt * r_k, axis=-1, keepdims=True) * vh
        finals.append(S_fin)
    mu = jnp.mean(y_sum, axis=-1, keepdims=True)
    var = jnp.mean(jnp.square(y_sum - mu), axis=-1, keepdims=True)
    yn = ((y_sum - mu) * lax.rsqrt(var + GN_EPS)).reshape(B, T, C_DIM)
    o = (yn * P['c_ln_w'][l].astype(f32) + P['c_ln_b'][l].astype(f32) + bonus.reshape(B, T, C_DIM)) * g
    return o.astype(dtype), jnp.stack(finals, axis=1)


def mixer(h, P, l, ctx_cache):
    B, T, _ = h.shape
    z = h @ P['w_in'][l]
    split_points = np.cumsum(IN_SPLITS)[:-1].tolist()
    aq, ak, av, bq, bk, bv, cr, ck, cv, cw, ca, cg = jnp.split(z, split_points, axis=-1)
    aq = aq.reshape(B, T, A_HEADS, HEAD_DIM)
    ak = ak.reshape(B, T, A_KV_HEADS, HEAD_DIM)
    av = av.reshape(B, T, A_KV_HEADS, HEAD_DIM)
    bq = rms_norm(bq.reshape(B, T, B_HEADS, HEAD_DIM), P['b_q_norm'][l])
    bk = rms_norm(bk.reshape(B, T, B_KV_HEADS, HEAD_DIM), P['b_k_norm'][l])
    bv = bv.reshape(B, T, B_KV_HEADS, HEAD_DIM)
    if ctx_cache is None:
        oa = dense_attn(aq, ak, av, P['a_sink'][l])
        ob = dense_attn(bq, bk, bv)
        init = jnp.zeros((B, 2, C_HEADS, HEAD_DIM, HEAD_DIM), jnp.float32)
        oc, st = rwkv_mix(cr, ck, cv, cw, ca, cg, P, l, init)
        cache = (ak, av, bk, bv, st)
    else:
        ka_c, va_c, kb_c, vb_c, st_c = ctx_cache
        oa = window_attn(rope_2d(aq), rope_2d(ak), av, ka_c, va_c, P['a_sink'][l])
        ob = dense_attn(rope_2d(bq),
                        jnp.concatenate([kb_c, rope_2d(bk)], axis=1),
                        jnp.concatenate([vb_c, bv], axis=1))
        oc, _ = rwkv_mix(cr, ck, cv, cw, ca, cg, P, l, st_c)
        cache = None
    o = jnp.concatenate([oa.reshape(B, T, A_DIM), ob.reshape(B, T, B_DIM), oc], axis=-1) @ P['w_out'][l]
    return o, cache


def trunk_layer(x, mod, P, l, ctx_cache):
    shift1, scale1, gate1, shift2, scale2, gate2 = jnp.split(mod, 6, axis=-1)
    h = rms_norm(x, P['norm_mix_pre'][l]) * (1.0 + scale1) + shift1
    o, cache = mixer(h, P, l, ctx_cache)
    x = x + gate1 * rms_norm(o, P['norm_mix_post'][l])
    h = rms_norm(x, P['norm_ffn_pre'][l]) * (1.0 + scale2) + shift2
    gu = h @ P['w_gu'][l]
    f = (jax.nn.silu(gu[..., :FF_DIM]) * gu[..., FF_DIM:]) @ P['w_down'][l]
    x = x + gate2 * rms_norm(f, P['norm_ffn_post'][l])
    return x, cache


def setup_inputs(seed: int = 0) -> dict:
    key = jax.random.key(seed)
    ks = iter(jax.random.split(key, 48))
    f32 = jnp.float32

    def nrm(shape, scale):
        return jax.random.normal(next(ks), shape, f32) * scale

    return {
        'x_prompt': nrm((BATCH, SEQ, D_MODEL), 1.0),
        'x_sample': nrm((DEC_BATCH, DEC_SEQ, D_MODEL), 1.0),
        'cache_a_k': nrm((DEC_BATCH, DEPTH, PAST_LEN, A_KV_HEADS, HEAD_DIM), 1.0),
        'cache_a_v': nrm((DEC_BATCH, DEPTH, PAST_LEN, A_KV_HEADS, HEAD_DIM), 1.0),
        'cache_b_k': nrm((DEC_BATCH, DEPTH, PAST_LEN, B_KV_HEADS, HEAD_DIM), 1.0),
        'cache_b_v': nrm((DEC_BATCH, DEPTH, PAST_LEN, B_KV_HEADS, HEAD_DIM), 1.0),
        'state_c': nrm((DEC_BATCH, DEPTH, 2, C_HEADS, HEAD_DIM, HEAD_DIM), 0.3),
        'c': nrm((DEC_BATCH, D_MODEL), 1.0),
        'c_ctx': nrm((D_MODEL,), 1.0),
        'w_mod': nrm((DEPTH, D_MODEL, 6 * D_MODEL), D_MODEL ** -0.5),
        'b_mod': nrm((DEPTH, 6 * D_MODEL), 0.02),
        'norm_mix_pre': 1.0 + nrm((DEPTH, D_MODEL), 0.05),
        'norm_mix_post': 1.0 + nrm((DEPTH, D_MODEL), 0.05),
        'norm_ffn_pre': 1.0 + nrm((DEPTH, D_MODEL), 0.05),
        'norm_ffn_post': 1.0 + nrm((DEPTH, D_MODEL), 0.05),
        'w_in': nrm((DEPTH, D_MODEL, IN_COLS), D_MODEL ** -0.5),
        'w_out': nrm((DEPTH, MIX_DIM, D_MODEL), MIX_DIM ** -0.5),
        'a_sink': nrm((DEPTH, A_HEADS), 0.5),
        'b_q_norm': 1.0 + nrm((DEPTH, HEAD_DIM), 0.05),
        'b_k_norm': 1.0 + nrm((DEPTH, HEAD_DIM), 0.05),
        'c_w0': jax.random.uniform(next(ks), (DEPTH, 2, C_DIM), f32, -5.0, 0.5),
        'c_w_up': nrm((DEPTH, 2, W_RANK, C_DIM), 0.1),
        'c_a0': nrm((DEPTH, 2, C_DIM), 0.1),
        'c_a_up': nrm((DEPTH, 2, A_RANK, C_DIM), 0.1),
        'c_g_up': nrm((DEPTH, G_RANK, C_DIM), G_RANK ** -0.5),
        'c_k_k': 0.85 + nrm((DEPTH, C_DIM), 0.05),
        'c_k_a': 1.0 + nrm((DEPTH, C_DIM), 0.05),
        'c_r_k': nrm((DEPTH, C_HEADS, HEAD_DIM), 0.1),
        'c_ln_w': 1.0 + nrm((DEPTH, C_DIM), 0.05),
        'c_ln_b': nrm((DEPTH, C_DIM), 0.02),
        'w_gu': nrm((DEPTH, D_MODEL, 2 * FF_DIM), D_MODEL ** -0.5),
        'w_down': nrm((DEPTH, FF_DIM, D_MODEL), FF_DIM ** -0.5),
    }


def reference(x_prompt, x_sample, cache_a_k, cache_a_v, cache_b_k, cache_b_v, state_c, c, c_ctx,
              w_mod, b_mod, norm_mix_pre, norm_mix_post, norm_ffn_pre, norm_ffn_post, w_in, w_out,
              a_sink, b_q_norm, b_k_norm, c_w0, c_w_up, c_a0, c_a_up, c_g_up, c_k_k, c_k_a, c_r_k,
              c_ln_w, c_ln_b, w_gu, w_down):
    P = {
        'norm_mix_pre': norm_mix_pre, 'norm_mix_post': norm_mix_post,
        'norm_ffn_pre': norm_ffn_pre, 'norm_ffn_post': norm_ffn_post,
        'w_in': w_in, 'w_out': w_out, 'a_sink': a_sink, 'b_q_norm': b_q_norm, 'b_k_norm': b_k_norm,
        'c_w0': c_w0, 'c_w_up': c_w_up, 'c_a0': c_a0, 'c_a_up': c_a_up, 'c_g_up': c_g_up,
        'c_k_k': c_k_k, 'c_k_a': c_k_a, 'c_r_k': c_r_k, 'c_ln_w': c_ln_w, 'c_ln_b': c_ln_b,
        'w_gu': w_gu, 'w_down': w_down,
    }
    xp = x_prompt
    ak_l, av_l, bk_l, bv_l, st_l = [], [], [], [], []
    for l in range(DEPTH):
        mod = (jax.nn.silu(c_ctx) @ w_mod[l] + b_mod[l])[None, None, :]
        xp, (ak, av, bk, bv, st) = trunk_layer(xp, mod, P, l, None)
        ak_l.append(ak)
        av_l.append(av)
        bk_l.append(bk)
        bv_l.append(bv)
        st_l.append(st.astype(x_prompt.dtype))
    xs = x_sample
    for l in range(DEPTH):
        mod = (jax.nn.silu(c) @ w_mod[l] + b_mod[l])[:, None, :]
        ctx = (cache_a_k[:, l], cache_a_v[:, l], cache_b_k[:, l], cache_b_v[:, l], state_c[:, l])
        xs, _ = trunk_layer(xs, mod, P, l, ctx)
    new_a_k = jnp.stack(ak_l, axis=1)
    new_a_v = jnp.stack(av_l, axis=1)
    new_b_k = jnp.stack(bk_l, axis=1)
    new_b_v = jnp.stack(bv_l, axis=1)
    new_state_c = jnp.stack(st_l, axis=1)
    return (xp, xs, new_a_k, new_a_v, new_b_k, new_b_v, new_state_c)
```

```python
import numpy as np
import concourse.bass as bass
import concourse.mybir as mybir
from concourse.bass_utils import run_bass_kernel_spmd

F32 = mybir.dt.float32
BF16 = mybir.dt.bfloat16
AF = mybir.ActivationFunctionType
ALU = mybir.AluOpType
AX = mybir.AxisListType

L = 2
NDMASEM = 12
SB_LO = 16512
SB_HI = 229344
PVL = 128


class Op:
    __slots__ = ("eng", "fn", "deps", "is_dma", "need_inc", "sem", "val")

    def __init__(self, eng, fn, deps, is_dma):
        self.eng, self.fn, self.deps, self.is_dma = eng, fn, deps, is_dma
        self.need_inc = is_dma
        self.sem = None
        self.val = None


def _real(e):
    return "pool" if e == "poolq" else e


class KB:
    def __init__(self):
        self.nc = bass.Bass("TRN2", target_bir_lowering=False)
        nc = self.nc
        self.engs = {"pe": nc.tensor, "act": nc.scalar, "dve": nc.vector, "pool": nc.gpsimd, "sp": nc.sync}
        self.ops = []
        self.last_w = {}
        self.readers = {}
        self._ctx = []
        self.off = SB_LO
        self.last_eng = {}
        self.recent_dma = {"sp": [], "poolq": []}
        self.pending = {}
        self.alt = 0
        self.last_rg = {}
        self.defer = None
        self.nuniq = 0

    def dram(self, name, shape, dtype, kind):
        return self.nc.dram_tensor(name, list(shape), dtype, kind=kind).ap()

    def alloc(self, shape, dtype, name=None):
        nb = 4 if dtype == F32 else 2
        n = 1
        for s in shape[1:]:
            n *= s
        size = (n * nb + 31) // 32 * 32
        assert self.off + size <= SB_HI, ("SBUF overflow", name, self.off, size)
        self.nuniq += 1
        shp = list(shape)
        part = shp[0]
        shp[0] = 128
        t = self.nc.alloc_sbuf_tensor_at(name or ("t%d" % self.nuniq), shp, dtype, offset=self.off)
        self.off += size
        if part != 128:
            return t[0:part]
        return t

    def ps(self, name, shape, dtype=F32):
        cm = self.nc.psum_tensor(name, list(shape), dtype)
        t = cm.__enter__()
        self._ctx.append(cm)
        return t

    def replay(self, lst, n):
        cnt = 0
        while lst and cnt < n:
            if lst[0][0] == "pe" and cnt > 0 and n < 1000:
                break
            eng, fn, r, w, rgv = lst.pop(0)
            self.op(eng, fn, r, w, rgv=rgv)
            cnt += 1
            if eng == "pe" and n < 1000:
                break

    def op(self, eng, fn, r=(), w=(), extra=(), rgv=None):
        if self.defer is not None:
            self.defer.append((eng, fn, tuple(r), tuple(w), rgv))
            return None
        if rgv is not None:
            extra = []
            for t in w:
                prev = self.last_rg.get(t)
                if prev is not None and prev[0] != rgv:
                    extra.append(prev[1])
        o = self._op(eng, fn, r, w, extra)
        if rgv is not None:
            for t in w:
                self.last_rg[t] = (rgv, o)
        return o

    def _op(self, eng, fn, r=(), w=(), extra=()):
        is_dma = eng in ("sp", "poolq")
        deps = []
        for t in r:
            lw = self.last_w.get(t)
            if lw is not None:
                deps.append(lw)
            if t[:2] in ("pb", "pt"):
                for rd in self.readers.get(t, ()):
                    if rd.eng != eng:
                        deps.append(rd)
        for t in w:
            lw = self.last_w.get(t)
            if lw is not None:
                deps.append(lw)
            deps.extend(self.readers.get(t, ()))
        real = _real(eng)
        fdeps = []
        seen = set()
        for d in extra:
            if id(d) not in seen:
                seen.add(id(d))
                fdeps.append(d)
        for d in deps:
            if id(d) in seen:
                continue
            seen.add(id(d))
            if not d.is_dma and not is_dma and _real(d.eng) == real:
                if real == "pe":
                    continue
            fdeps.append(d)
        pb = self.pending.pop(eng, None)
        if pb:
            for d in pb:
                if id(d) not in seen and not (not d.is_dma and not is_dma and _real(d.eng) == real):
                    seen.add(id(d))
                    fdeps.append(d)
        o = Op(eng, fn, fdeps, is_dma)
        self.ops.append(o)
        for t in r:
            self.readers.setdefault(t, []).append(o)
        for t in w:
            self.last_w[t] = o
            self.readers[t] = []
        if is_dma:
            lst = self.recent_dma[eng]
            lst.append(o)
            if len(lst) > NDMASEM:
                lst.pop(0)
        else:
            self.last_eng[eng] = o
        return o

    def barrier(self):
        snap = list(self.last_eng.values()) + self.recent_dma["sp"] + self.recent_dma["poolq"]
        for e in ("pe", "act", "dve", "pool", "sp", "poolq"):
            self.pending[e] = list(snap) + self.pending.get(e, [])

    def mm(self, out, lhsT, rhs, start=True, stop=True, tp=None, r=(), w=()):
        rg = int(lhsT.start_partition())
        if tp is None:
            return self.op("pe", lambda e: e.matmul(out, lhsT=lhsT, rhs=rhs, start=start, stop=stop), r, w, rgv=rg)
        return self.op("pe", lambda e: e.matmul(out, lhsT=lhsT, rhs=rhs, start=start, stop=stop, tile_position=tp), r, w, rgv=rg)

    def tr(self, out, in_, ident, r=(), w=()):
        rg = int(in_.start_partition())
        return self.op("pe", lambda e: e.transpose(out, in_, ident), r, w, rgv=rg)

    def actf(self, out, in_, func, bias=None, scale=None, r=(), w=()):
        kw = {}
        if bias is not None:
            kw["bias"] = bias
        if scale is not None:
            kw["scale"] = scale
        return self.op("act", lambda e: e.activation(out=out, in_=in_, func=func, **kw), r, w)

    def tt(self, eng, out, in0, in1, op, r=(), w=()):
        return self.op(eng, lambda e: e.tensor_tensor(out=out, in0=in0, in1=in1, op=op), r, w)

    def ts(self, eng, out, in0, s1, s2, op0, op1=None, r=(), w=()):
        if op1 is None:
            return self.op(eng, lambda e: e.tensor_scalar(out=out, in0=in0, scalar1=s1, scalar2=None, op0=op0), r, w)
        return self.op(eng, lambda e: e.tensor_scalar(out=out, in0=in0, scalar1=s1, scalar2=s2, op0=op0, op1=op1), r, w)

    def stt(self, out, in0, scalar, in1, op0, op1, r=(), w=()):
        return self.op("dve", lambda e: e.scalar_tensor_tensor(out=out, in0=in0, scalar=scalar, in1=in1, op0=op0, op1=op1), r, w)

    def cp(self, eng, out, in_, r=(), w=()):
        if eng == "any":
            self.alt ^= 1
            eng = "act" if self.alt else "dve"
        if eng == "act":
            return self.op("act", lambda e: e.copy(out=out, in_=in_), r, w)
        return self.op(eng, lambda e: e.tensor_copy(out=out, in_=in_), r, w)

    def recip(self, out, in_, r=(), w=()):
        if FAST_RECIP:
            return self.op("dve", lambda e: e.reciprocal_approx_fast(out, in_), r, w)
        return self.op("dve", lambda e: e.reciprocal(out=out, in_=in_), r, w)

    def memset(self, eng, ap, val, r=(), w=()):
        return self.op(eng, lambda e: e.memset(ap, val), r, w)

    def dma(self, out, in_, r=(), w=(), q="sp"):
        return self.op(q, lambda e: e.dma_start(out=out, in_=in_), r, w)

    def emit(self):
        nc = self.nc
        for o in self.ops:
            for d in o.deps:
                d.need_inc = True
        sems = {}
        for e in ("pe", "act", "dve", "pool"):
            cm = nc.semaphore("s_" + e)
            sems[e] = cm.__enter__()
            self._ctx.append(cm)
        dsem = {}
        for q in ("sp", "poolq"):
            lst = []
            for i in range(NDMASEM):
                cm = nc.semaphore("d_%s_%d" % (q, i))
                lst.append(cm.__enter__())
                self._ctx.append(cm)
            dsem[q] = lst
        cnt = {e: 0 for e in sems}
        dcnt = {q: [0] * NDMASEM for q in dsem}
        dnext = {"sp": 0, "poolq": 0}
        for o in self.ops:
            if o.is_dma:
                k = dnext[o.eng]
                dnext[o.eng] = (k + 1) % NDMASEM
                o.sem = dsem[o.eng][k]
                o.val = dcnt[o.eng][k] + 16
                dcnt[o.eng][k] = o.val
            elif o.need_inc:
                cnt[o.eng] += 1
                o.sem = sems[o.eng]
                o.val = cnt[o.eng]
        waited = {}
        nwait = 0
        for o in self.ops:
            real = _real(o.eng)
            E = self.engs[real]
            if o.is_dma and o.val > 16:
                key = (real, id(o.sem))
                if waited.get(key, 0) < o.val - 16:
                    E.wait_ge(o.sem, o.val - 16)
                    waited[key] = o.val - 16
                    nwait += 1
            for d in o.deps:
                key = (real, id(d.sem))
                if waited.get(key, 0) < d.val:
                    E.wait_ge(d.sem, d.val)
                    waited[key] = d.val
                    nwait += 1
            ins = o.fn(E)
            if o.need_inc:
                ins.then_inc(o.sem, 16 if o.is_dma else 1)
        for q in ("sp", "poolq"):
            for k in range(NDMASEM):
                if dcnt[q][k] > 0:
                    self.engs[_real(q)].wait_ge(dsem[q][k], dcnt[q][k])
        self.stats = dict(n_ops=len(self.ops), n_wait=nwait)
        return nc


O_AQ, O_AK, O_AV, O_BQ, O_BK, O_BV, O_CR, O_CK, O_CV, O_CW, O_CA, O_CG = (
    0, 256, 384, 512, 896, 1024, 1152, 1536, 1920, 2304, 2368, 2432)


def _win_cols():
    def hd(base, h):
        return list(range(base + 64 * h, base + 64 * h + 64))

    def rot(c):
        return c[32:] + c[:32]

    pairs = [(hd(O_AQ, 0), hd(O_AQ, 1)), (hd(O_AQ, 2), hd(O_AQ, 3)),
             (hd(O_AK, 0), hd(O_AK, 0)), (hd(O_AK, 1), hd(O_AK, 1)),
             (hd(O_BQ, 0), hd(O_BQ, 1)), (hd(O_BQ, 2), hd(O_BQ, 3)), (hd(O_BQ, 4), hd(O_BQ, 5)),
             (hd(O_BK, 0), hd(O_BK, 0)), (hd(O_BK, 0), hd(O_BK, 1)), (hd(O_BK, 1), hd(O_BK, 1))]
    cols = []
    for a, b in pairs:
        cols += a + b
        cols += rot(a) + rot(b)
    cols += list(range(O_CR, O_CR + 384)) + list(range(O_CK, O_CK + 384)) + list(range(O_CV, O_CV + 384))
    cols += list(range(O_CW, O_CW + 128)) + list(range(O_CG, O_CG + 128))
    cols += list(range(O_AV, O_AV + 128)) + list(range(O_BV, O_BV + 128)) + list(range(O_CV, O_CV + 384))
    assert len(cols) == 36 * 128
    return np.array(cols)


def _fm(v):
    return np.ascontiguousarray(v.reshape(8, 128).T)


def _pair(v384):
    return np.ascontiguousarray(v384.reshape(3, 128).T)


def _prep_shared(I):
    f = np.float32
    S = {}
    wc = _win_cols()
    wmod = np.empty((L, 128, 12 * 4096), f)
    win = np.empty((L, 128, 9 * 4096), f)
    wout = np.empty((L, 128, 2 * 4096), f)
    wgu = np.empty((L, 128, 11 * 4096), f)
    wdn = np.empty((L, 128, 8 * 2816), f)
    for l in range(L):
        wmod[l] = I["w_mod"][l].reshape(8, 128, 12, 512).transpose(1, 2, 0, 3).reshape(128, -1)
        win[l] = I["w_in"][l][:, wc].reshape(8, 128, 9, 512).transpose(1, 2, 0, 3).reshape(128, -1)
        wout[l] = I["w_out"][l].reshape(8, 128, 2, 4, 128).transpose(1, 2, 3, 0, 4).reshape(128, -1)
        g = I["w_gu"][l][:, :2816].reshape(8, 128, 11, 2, 128)
        u = I["w_gu"][l][:, 2816:].reshape(8, 128, 11, 2, 128)
        gu = np.stack([g, u], axis=4)
        wgu[l] = gu.transpose(1, 2, 0, 3, 4, 5).reshape(128, -1)
        wdn[l] = I["w_down"][l].reshape(22, 128, 8, 128).transpose(1, 2, 0, 3).reshape(128, -1)
    S.update(wmod=wmod, win=win, wout=wout, wgu=wgu, wdn=wdn)
    pv = np.zeros((128, L * PVL), f)
    for l in range(L):
        b = l * PVL
        pv[:, b + 0:b + 8] = _fm(I["norm_mix_pre"][l])
        pv[:, b + 8:b + 16] = _fm(I["norm_mix_post"][l])
        pv[:, b + 16:b + 24] = _fm(I["norm_ffn_pre"][l])
        pv[:, b + 24:b + 32] = _fm(I["norm_ffn_post"][l])
        pv[:, b + 32:b + 80] = np.ascontiguousarray(I["b_mod"][l].reshape(48, 128).T)
        gq, gk = I["b_q_norm"][l], I["b_k_norm"][l]
        rot = lambda v: np.concatenate([v[32:], v[:32]])
        pv[:, b + 80] = np.tile(gq, 2)
        pv[:, b + 81] = np.tile(gk, 2)
        pv[:, b + 82] = np.tile(rot(gq), 2)
        pv[:, b + 83] = np.tile(rot(gk), 2)
        pv[:, b + 84:b + 88] = I["a_sink"][l][None, :]
        for d in range(2):
            pv[:, b + 88 + 3 * d:b + 91 + 3 * d] = _pair(I["c_w0"][l, d])
            pv[:, b + 94 + 3 * d:b + 97 + 3 * d] = _pair(I["c_a0"][l, d])
        pv[:, b + 100:b + 103] = _pair(I["c_k_k"][l])
        pv[:, b + 103:b + 106] = _pair(I["c_k_a"][l])
        pv[:, b + 106:b + 109] = _pair(I["c_r_k"][l].reshape(384))
        pv[:, b + 109:b + 112] = _pair(I["c_ln_w"][l])
        pv[:, b + 112:b + 115] = _pair(I["c_ln_b"][l])
    S["pv"] = pv
    wlr = np.empty((128, L, 2, 384), f)
    wg = np.empty((128, L, 384), f)
    for l in range(L):
        for d in range(2):
            wlr[0:64, l, d] = I["c_w_up"][l, d]
            wlr[64:128, l, d] = I["c_a_up"][l, d]
        wg[:, l] = I["c_g_up"][l]
    S["wlr"] = wlr
    S["wgup"] = wg
    T = 1024
    row = np.repeat(np.arange(T // 64), 64).astype(f)
    col = (np.arange(T) % 64).astype(f)
    fr = (np.float32(10000.0) ** (-np.arange(16, dtype=f) / np.float32(16))).astype(f)
    ang = np.concatenate([row[:, None] * fr, col[:, None] * fr], axis=-1).astype(f)
    cs, sn = np.cos(ang).astype(f), np.sin(ang).astype(f)
    C64 = np.concatenate([cs, cs], axis=1).T
    S64 = np.concatenate([-sn, sn], axis=1).T
    rope = np.empty((128, 2, T), f)
    rope[:, 0] = np.tile(C64, (2, 1))
    rope[:, 1] = np.tile(S64, (2, 1))
    S["rope"] = rope
    bb = np.arange(128)[:, None]
    qq = np.arange(512)[None, :]
    mb = np.empty((128, 6, 512), f)
    for j in range(6):
        mb[:, j] = (np.abs((j - 1) * 128 + bb - qq) <= 128).astype(f)
    S["maskb"] = mb
    r_ = np.arange(64)[:, None]
    c_ = np.arange(64)[None, :]
    rm = np.zeros((128, 2, 640), f)
    for d in range(2):
        up = (c_ > r_) if d == 0 else (c_ < r_)
        upe = (c_ >= r_) if d == 0 else (c_ <= r_)
        lo = (c_ < r_) if d == 0 else (c_ > r_)
        for j, m in enumerate([up, lo, up, upe, upe]):
            rm[0:64, d, j * 128:j * 128 + 64] = m
            rm[64:128, d, j * 128 + 64:j * 128 + 128] = m
    S["rmask"] = rm
    S["ident"] = np.eye(128, dtype=f)
    sm = np.ones((128, 2, 128), f)
    sm[:, 0, 0::64] = 0.0
    sm[:, 1, 63::64] = 0.0
    S["scanm"] = sm
    ob = np.zeros((128, 128), f)
    ob[:64, :64] = 1.0
    ob[64:, 64:] = 1.0
    S["onesbd"] = ob
    return S


def _prep_core(I, i):
    f = np.float32
    C = {}
    xp = I["x_prompt"][4 * i:4 * i + 4].reshape(1024, 1024)
    C["xp"] = np.ascontiguousarray(xp.T.reshape(8, 128, 1024).transpose(1, 0, 2))
    b = i % 2
    C["xs"] = np.ascontiguousarray(I["x_sample"][b].T.reshape(8, 128, 1024).transpose(1, 0, 2))
    cv = np.empty((128, 8, 2), f)
    cv[:, :, 0] = _fm(I["c_ctx"])
    cv[:, :, 1] = _fm(I["c"][b])
    C["cvec"] = cv
    ak = I["cache_a_k"][b]
    bk = I["cache_b_k"][b]
    cak = np.empty((128, L, 2, 256), f)
    cbk = np.empty((128, L, 3, 256), f)
    for l in range(L):
        for kv in range(2):
            kt = ak[l, :, kv, :].T
            cak[0:64, l, kv] = kt
            cak[64:128, l, kv] = kt
        k0, k1 = bk[l, :, 0, :].T, bk[l, :, 1, :].T
        cbk[0:64, l, 0], cbk[64:128, l, 0] = k0, k0
        cbk[0:64, l, 1], cbk[64:128, l, 1] = k0, k1
        cbk[0:64, l, 2], cbk[64:128, l, 2] = k1, k1
    C["cak"], C["cbk"] = cak, cbk
    C["cav"] = np.ascontiguousarray(I["cache_a_v"][b].reshape(L, 2, 128, 2, 64).transpose(2, 0, 1, 3, 4))
    C["cbv"] = np.ascontiguousarray(I["cache_b_v"][b].reshape(L, 2, 128, 2, 64).transpose(2, 0, 1, 3, 4))
    st = I["state_c"][b]
    C["st0"] = np.ascontiguousarray(st.reshape(L, 2, 3, 2, 64, 64).transpose(3, 5, 0, 1, 2, 4).reshape(128, L, 2, 3, 64))
    return C


_KBREF = [None]


class _Stop(Exception):
    pass


FAST_RECIP = False
STOP = [None]
DBG = set()
_seen_tags = []


def stage(tag):
    if tag not in _seen_tags:
        _seen_tags.append(tag)
        if STOP[0] == tag:
            raise _Stop()


def build():
    del _seen_tags[:]
    try:
        return _build()
    except _Stop:
        kb = _KBREF[0]
        kb.emit()
        return kb


def _build():
    kb = KB()
    _KBREF[0] = kb
    A = kb.alloc
    din = lambda n, s: kb.dram(n, s, F32, "ExternalInput")
    dout = lambda n, s: kb.dram(n, s, F32, "ExternalOutput")
    xin = [din("xp", [128, 8, 1024]), din("xs", [128, 8, 1024])]
    cvec_d = din("cvec", [128, 8, 2])
    pv_d = din("pv", [128, L * PVL])
    rope_d = din("rope", [128, 2, 1024])
    maskb_d = din("maskb", [128, 6, 512])
    rmask_d = din("rmask", [128, 2, 640])
    ident_d = din("ident", [128, 128])
    scanm_d = din("scanm", [128, 2, 128])
    onesbd_d = din("onesbd", [128, 128])
    cak_d = din("cak", [128, L, 2, 256])
    cbk_d = din("cbk", [128, L, 3, 256])
    cav_d = din("cav", [128, L, 2, 2, 64])
    cbv_d = din("cbv", [128, L, 2, 2, 64])
    st0_d = din("st0", [128, L, 2, 3, 64])
    wmod_d = din("wmod", [L, 128, 12 * 4096])
    win_d = din("win", [L, 128, 9 * 4096])
    wout_d = din("wout", [L, 128, 2 * 4096])
    wgu_d = din("wgu", [L, 128, 11 * 4096])
    wdn_d = din("wdn", [L, 128, 8 * 2816])
    wlr_d = din("wlr", [128, L, 2, 384])
    wgup_d = din("wgup", [128, L, 384])
    yo = [dout("yp", [128, 8, 1024]), dout("ys", [128, 8, 1024])]
    oak_d = dout("oak", [64, L, 2, 1024])
    obk_d = dout("obk", [128, L, 1024])
    oav_d = dout("oav", [128, L, 8, 128])
    obv_d = dout("obv", [128, L, 8, 128])
    ost_d = dout("ost", [128, L, 4, 2, 3, 64])

    xT = A([128, 8, 1024], F32, "xT")
    pv = A([128, L * PVL], F32, "pv")
    ropeT = A([128, 2, 1024], F32, "rope")
    maskb = A([128, 6, 512], BF16, "maskb")
    rmask = A([128, 2, 640], BF16, "rmask")
    identf = A([128, 128], F32, "identf")
    identb = A([128, 128], BF16, "identb")
    scanm = A([128, 2, 128], F32, "scanm")
    onesbd = A([128, 128], BF16, "onesbd")
    ones128 = A([128, 128], BF16, "ones128")
    ones64 = A([128, 64], BF16, "ones64")
    cvt = A([128, 8, 2], F32, "cvt")
    sct = A([128, 8, 2], BF16, "sct")
    modT = [A([128, 48, 2], F32, "modT%d" % l) for l in range(L)]
    MD = [[A([128, 6, 8], F32, "MD%d%d" % (l, ph)) for ph in range(2)] for l in range(L)]
    esink = A([128, L * 4], F32, "esink")
    omka = A([128, L * 3], F32, "omka")
    stg = [A([128, 4096], BF16, "stg%d" % i) for i in range(2)]
    hT = A([128, 8, 1024], BF16, "hT")
    oT = hT
    cak = A([128, L, 2, 256], BF16, "cak")
    cbk = A([128, L, 3, 256], BF16, "cbk")
    cav = A([128, L, 2, 2, 64], BF16, "cav")
    cbv = A([128, L, 2, 2, 64], BF16, "cbv")
    zc_off = kb.off
    zc = [A([128, 1024], BF16, "zc%d" % i) for i in range(11)]
    p2_off = kb.off
    vtok = A([128, 8, 256], BF16, "vtok")
    vtokc = A([64, 16, 384], BF16, "vtokc")
    wlr = A([128, L, 2, 384], BF16, "wlr")
    wgup = A([128, L, 384], BF16, "wgup")
    W0 = kb.off

    PB = [kb.ps("pb%d" % i, [128, 512], F32) for i in range(7)]
    PT = kb.ps("pt", [128, 1024], BF16)
    pbn = ["pb%d" % i for i in range(7)]

    kb.dma(pv[:], pv_d, w=["pv"])
    kb.dma(ropeT[:], rope_d, w=["rope"])
    kb.dma(maskb[:], maskb_d, w=["maskb"], q="poolq")
    kb.dma(rmask[:], rmask_d, w=["rmask"], q="poolq")
    kb.dma(identf[:], ident_d, w=["identf"])
    kb.dma(identb[:], ident_d, w=["identb"], q="poolq")
    kb.dma(scanm[:], scanm_d, w=["scanm"])
    kb.dma(onesbd[:], onesbd_d, w=["onesbd"], q="poolq")
    kb.dma(cvt[:], cvec_d, w=["cvt"])
    kb.dma(cak[:], cak_d, w=["cak"], q="poolq")
    kb.dma(cbk[:], cbk_d, w=["cbk"], q="poolq")
    kb.dma(cav[:], cav_d, w=["cav"], q="poolq")
    kb.dma(cbv[:], cbv_d, w=["cbv"], q="poolq")
    kb.dma(wlr[:], wlr_d, w=["wlr"], q="poolq")
    kb.dma(wgup[:], wgup_d, w=["wgup"], q="poolq")
    kb.memset("dve", ones128[:], 1.0, w=["ones128"])
    kb.memset("dve", ones64[:], 1.0, w=["ones64"])
    kb.actf(sct[:], cvt[:], AF.Silu, r=["cvt"], w=["sct"])
    for l in range(L):
        b = l * PVL
        kb.actf(esink[:, l * 4:l * 4 + 4], pv[:, b + 84:b + 88], AF.Exp, r=["pv"], w=["esink"])
        kb.ts("dve", omka[:, l * 3:l * 3 + 3], pv[:, b + 103:b + 106], -1.0, 1.0, ALU.mult, ALU.add, r=["pv"], w=["omka"])
    stage("consts")

    st_i = [0]

    def slab(src, n):
        i = st_i[0]
        st_i[0] ^= 1
        kb.dma(stg[i][:, 0:n], src, w=["stg%d" % i], q="poolq")
        return stg[i], "stg%d" % i

    bank_i = [0]

    def nbank():
        i = bank_i[0]
        bank_i[0] = (i + 1) % 4
        return PB[i], pbn[i]

    pmod = PB[5]

    def mods_slab(l, s):
        if True:
            sg_, tok = slab(wmod_d[l][:, s * 4096:(s + 1) * 4096], 4096)
            v = sg_[:, :].rearrange("p (k c) -> p k c", c=512)
            for q in range(4):
                mb_ = s * 4 + q
                for k in range(8):
                    kb.mm(pmod[:, mb_ * 2:mb_ * 2 + 2], lhsT=v[:, k, q * 128:(q + 1) * 128], rhs=sct[:, k, :],
                          start=(k == 0), stop=(k == 7), r=[tok, "sct"], w=["pb5"])

    def mods_final(l):
        pmv = pmod[:, 0:96].rearrange("p (m t) -> p m t", t=2)
        b = l * PVL
        for ph in range(2):
            kb.tt("dve", modT[l][:, :, ph], pmv[:, :, ph], pv[:, b + 32:b + 80], ALU.add, r=["pb5", "pv"], w=["modT%d" % l])
            m = MD[l][ph]
            mt = modT[l]
            tk = "MD%d%d" % (l, ph)
            kb.stt(m[:, 0, :], mt[:, 8:16, ph], 1.0, pv[:, b + 0:b + 8], ALU.add, ALU.mult, r=["modT%d" % l, "pv"], w=[tk])
            kb.cp("dve", m[:, 1, :], mt[:, 0:8, ph], r=["modT%d" % l], w=[tk])
            kb.tt("dve", m[:, 2, :], mt[:, 16:24, ph], pv[:, b + 8:b + 16], ALU.mult, r=["modT%d" % l, "pv"], w=[tk])
            kb.stt(m[:, 3, :], mt[:, 32:40, ph], 1.0, pv[:, b + 16:b + 24], ALU.add, ALU.mult, r=["modT%d" % l, "pv"], w=[tk])
            kb.cp("dve", m[:, 4, :], mt[:, 24:32, ph], r=["modT%d" % l], w=[tk])
            kb.tt("dve", m[:, 5, :], mt[:, 40:48, ph], pv[:, b + 24:b + 32], ALU.mult, r=["modT%d" % l, "pv"], w=[tk])


    for s_ in range(12):
        mods_slab(0, s_)
    mods_final(0)
    HS = [slice(0, 512), slice(512, 1024)]
    stage("mods")

    def run_layer(ph, l):
        b = l * PVL
        md = MD[l][ph]
        mdk = "MD%d%d" % (l, ph)
        roped = (ph == 1)
        kb.barrier()
        kb.off = W0
        tmpf = [A([128, 512], F32) for _ in range(2)]
        ropet = [A([128, 512], F32) for _ in range(2)]
        tmpb = [A([128, 512], F32) for _ in range(2)]
        rs = [A([128, 512], F32) for _ in range(2)]
        sd = A([128, 512], F32)
        sqb = [A([128, 512], BF16) for _ in range(2)]
        zq = [A([128, 1024], BF16) for _ in range(10)]
        pbuf = [A([128, 512], BF16) for _ in range(3)]
        dtmp = A([128, 512], F32)
        rD = A([128, 512], F32)
        akf = A([128, 512], F32)
        vf = A([128, 128], F32)
        uid = [0]

        def U():
            uid[0] += 1
            return "u%d_%d_%d" % (ph, l, uid[0])

        def norm_stats(srcs, src_toks, half, nfeat, ones_t, ones_tok, eps, rs_t, rs_tok):
            pn, pnn = PB[4], "pb4"
            n = len(srcs)
            for k in range(n):
                sq = sqb[k % 2]
                kb.actf(sq[:], srcs[k], AF.Square, r=[src_toks[k]], w=["sqb%d" % (k % 2)])
                kb.mm(pn[:], lhsT=ones_t, rhs=sq[:], start=(k == 0), stop=(k == n - 1),
                      r=["sqb%d" % (k % 2), ones_tok], w=[pnn])
            kb.actf(sd[:], pn[:], AF.Ln, bias=eps, scale=1.0 / nfeat, r=[pnn], w=["sd"])
            kb.actf(rs_t[:], sd[:], AF.Exp, scale=-0.5, r=["sd"], w=[rs_tok])

        for half in range(2):
            hs = HS[half]
            norm_stats([xT[:, k, hs] for k in range(8)], ["x%d_%d" % (k, half) for k in range(8)], half,
                       1024.0, ones128[:], "ones128", 1e-6, rs[0], "rs0")
            for k in range(8):
                t = tmpf[k % 2]
                kb.stt(t[:], xT[:, k, hs], md[:, 0, k:k + 1], rs[0][:], ALU.mult, ALU.mult,
                       r=["x%d_%d" % (k, half), mdk, "rs0"], w=["tmpf%d" % (k % 2)])
                kb.actf(hT[:, k, hs], t[:], AF.Identity, bias=md[:, 1, k:k + 1], scale=1.0,
                        r=["tmpf%d" % (k % 2), mdk], w=["h%d_%d" % (k, half)])

        hall = ["h%d_%d" % (k, hf) for k in range(8) for hf in range(2)]
        stage("prenorm")

        def in_block(sv, tok, q, half):
            bk_, bn = nbank()
            for k in range(8):
                kb.mm(bk_[:], lhsT=sv[:, k, q * 128:(q + 1) * 128], rhs=hT[:, k, HS[half]],
                      start=(k == 0), stop=(k == 7), r=[tok, "h%d_%d" % (k, half)], w=[bn])
            return bk_, bn

        for s in range(9):
            stage("win_s%d" % s)
            sg_, tok = slab(win_d[l][:, s * 4096:(s + 1) * 4096], 4096)
            sv = sg_[:, :].rearrange("p (k c) -> p k c", c=512)
            for q in range(4):
                blk = 4 * s + q
                if blk < 20:
                    p, isrot = blk // 2, blk % 2
                    if isrot and not roped:
                        continue
                    isB = p >= 4
                    gcol = b + (80 if p < 7 else 81) + (2 if isrot else 0)
                    for half in range(2):
                        hs = HS[half]
                        bk_, bn = in_block(sv, tok, q, half)
                        zt = "zq%d_%d" % (p, half)
                        if not isB:
                            if not roped:
                                kb.cp("any", zq[p][:, hs], bk_[:], r=[bn], w=[zt])
                                if p in (2, 3) and "noakcp" not in DBG:
                                    kb.cp("dve" if "akdve" in DBG else "any", akf[0:64, :], bk_[0:64, :], r=[bn], w=["akf"])
                                    if "noakdma" not in DBG:
                                        kb.dma(oak_d[:, l, p - 2, hs], akf[0:64, :], r=["akf"])
                            elif not isrot:
                                kb.tt("dve", ropet[half][:], bk_[:], ropeT[:, 0, hs], ALU.mult, r=[bn, "rope"], w=["ropet%d" % half])
                            else:
                                kb.tt("dve", tmpf[half][:], bk_[:], ropeT[:, 1, hs], ALU.mult, r=[bn, "rope"], w=["tmpf%d" % half])
                                kb.tt("pool", zq[p][:, hs], tmpf[half][:], ropet[half][:], ALU.add,
                                      r=["tmpf%d" % half, "ropet%d" % half], w=[zt])
                        else:
                            if not isrot:
                                kb.cp("act", tmpb[half][:], bk_[:], r=[bn], w=["tmpb%d" % half])
                                norm_stats([bk_[:]], [bn], half, 64.0, onesbd[:], "onesbd", 1e-6, rs[half], "rsB%d" % half)
                                if not roped:
                                    if p == 8:
                                        kb.stt(akf[:], tmpb[half][:], pv[:, gcol:gcol + 1], rs[half][:], ALU.mult, ALU.mult,
                                               r=["tmpb%d" % half, "rsB%d" % half, "pv"], w=["akf"])
                                        kb.dma(obk_d[:, l, hs], akf[:], r=["akf"])
                                        kb.cp("any", zq[p][:, hs], akf[:], r=["akf"], w=[zt])
                                    else:
                                        kb.stt(zq[p][:, hs], tmpb[half][:], pv[:, gcol:gcol + 1], rs[half][:], ALU.mult, ALU.mult,
                                               r=["tmpb%d" % half, "rsB%d" % half, "pv"], w=[zt])
                                else:
                                    kb.stt(tmpb[half][:], tmpb[half][:], pv[:, gcol:gcol + 1], rs[half][:], ALU.mult, ALU.mult,
                                           r=["tmpb%d" % half, "rsB%d" % half, "pv"], w=["tmpb%d" % half])
                                    kb.tt("pool", ropet[half][:], tmpb[half][:], ropeT[:, 0, hs], ALU.mult,
                                          r=["tmpb%d" % half, "rope"], w=["ropet%d" % half])
                            else:
                                kb.stt(tmpf[half][:], bk_[:], pv[:, gcol:gcol + 1], rs[half][:], ALU.mult, ALU.mult,
                                       r=[bn, "rsB%d" % half, "pv"], w=["tmpf%d" % half])
                                kb.tt("pool", tmpf[half][:], tmpf[half][:], ropeT[:, 1, hs], ALU.mult,
                                      r=["tmpf%d" % half, "rope"], w=["tmpf%d" % half])
                                kb.tt("dve", zq[p][:, hs], tmpf[half][:], ropet[half][:], ALU.add,
                                      r=["tmpf%d" % half, "ropet%d" % half], w=[zt])
                elif blk < 31:
                    ci = blk - 20
                    for half in range(2):
                        bk_, bn = in_block(sv, tok, q, half)
                        kb.cp("any", zc[ci][:, HS[half]], bk_[:], r=[bn], w=["zc%d_%d" % (ci, half)])
                elif blk == 31:
                    for tb in range(8):
                        bk_, bn = nbank()
                        for k in range(8):
                            kb.mm(bk_[:, 0:128], lhsT=hT[:, k, tb * 128:(tb + 1) * 128], rhs=sv[:, k, 384:512],
                                  start=(k == 0), stop=(k == 7), r=[tok, "h%d_%d" % (k, tb // 4)], w=[bn])
                        if not roped:
                            kb.cp("any", vf[:], bk_[:, 0:128], r=[bn], w=["vf"])
                            kb.dma(oav_d[:, l, tb, :], vf[:], r=["vf"])
                        kb.cp("any", vtok[:, tb, 0:128], bk_[:, 0:128], r=[bn], w=["vtok%d" % tb])
                elif blk == 32:
                    for tb in range(8):
                        bk_, bn = nbank()
                        for k in range(8):
                            kb.mm(bk_[:, 0:128], lhsT=hT[:, k, tb * 128:(tb + 1) * 128], rhs=sv[:, k, 0:128],
                                  start=(k == 0), stop=(k == 7), r=[tok, "h%d_%d" % (k, tb // 4)], w=[bn])
                        if not roped:
                            kb.cp("any", vf[:], bk_[:, 0:128], r=[bn], w=["vf"])
                            kb.dma(obv_d[:, l, tb, :], vf[:], r=["vf"])
                        kb.cp("any", vtok[:, tb, 128:256], bk_[:, 0:128], r=[bn], w=["vtok%d" % tb])

        zqall = lambda p: ["zq%d_0" % p, "zq%d_1" % p]
        stage("win_done")

        def attn_run(qi, hh, kt_ap_fn, srcs, qcols, ochunk, sink_col):
            hb = slice(hh * 64, hh * 64 + 64)
            nq = qcols.stop - qcols.start
            qhalf_toks = zqall(qi)
            pO, pOn = PB[2], "pb2"
            pD, pDn = PB[3], "pb3"
            n = len(srcs)
            for i, (kap, ktoks, vap, vtoks, mask) in enumerate(srcs):
                bs_, bsn = (PB[0], "pb0") if i % 2 == 0 else (PB[1], "pb1")
                kb.mm(bs_[:, 0:nq], lhsT=kap, rhs=zq[qi][hb, qcols], r=list(ktoks) + qhalf_toks, w=[bsn])
                P = pbuf[i % 3]
                pt = "pbuf%d" % (i % 3)
                kb.actf(P[:, 0:nq], bs_[:, 0:nq], AF.Exp, scale=0.125, r=[bsn], w=[pt])
                if mask is not None:
                    kb.tt("pool", P[:, 0:nq], P[:, 0:nq], mask, ALU.mult, r=[pt, "maskb"], w=[pt])
                tp = (0, 64) if hh else None
                kb.mm(pO[hb, 0:nq], lhsT=vap, rhs=P[:, 0:nq], start=(i == 0), stop=(i == n - 1), tp=tp,
                      r=[pt] + list(vtoks), w=[pOn])
                kb.mm(pD[hb, 0:nq], lhsT=ones64[:], rhs=P[:, 0:nq], start=(i == 0), stop=(i == n - 1), tp=tp,
                      r=[pt, "ones64"], w=[pDn])
            if sink_col is not None:
                kb.actf(dtmp[hb, 0:nq], pD[hb, 0:nq], AF.Ln, bias=esink[hb, sink_col:sink_col + 1], scale=1.0,
                        r=[pDn, "esink"], w=["dtmp"])
            else:
                kb.actf(dtmp[hb, 0:nq], pD[hb, 0:nq], AF.Ln, r=[pDn], w=["dtmp"])
            kb.actf(rD[hb, 0:nq], dtmp[hb, 0:nq], AF.Exp, scale=-1.0, r=["dtmp"], w=["rD"])
            kb.tt("dve", oT[hb, ochunk, qcols], pO[hb, 0:nq], rD[hb, 0:nq], ALU.mult, r=[pOn, "rD"],
                  w=["o%d_%d" % (ochunk, qcols.start // 512)])

        qdefs = [(0, 2, True, (0, 0), 0), (1, 3, True, (64, 64), 1),
                 (4, 7, False, (128, 128), 0), (5, 8, False, (128, 192), 1), (6, 9, False, (192, 192), 2)]
        kb.barrier()
        for oi, (qi, ki, isA, vcols, cidx) in enumerate(qdefs):
            for hh in range(2):
                hb = slice(hh * 64, hh * 64 + 64)
                vc = vcols[hh]
                sink_col = (l * 4 + (qi * 2 + hh)) if isA else None
                if not roped:
                    for sq in range(4):
                        qcols = slice(sq * 256, sq * 256 + 256)
                        srcs = []
                        for j in range(2):
                            tb = 2 * sq + j
                            srcs.append((zq[ki][hb, tb * 128:(tb + 1) * 128], zqall(ki),
                                         vtok[:, tb, vc:vc + 64], ["vtok%d" % tb], None))
                        attn_run(qi, hh, None, srcs, qcols, oi, sink_col)
                else:
                    for hf in range(2):
                        qcols = HS[hf]
                        srcs = []
                        for cc in range(2):
                            if isA:
                                kap = cak[hb, l, cidx, cc * 128:(cc + 1) * 128]
                                vap = cav[:, l, cc, vc // 64, :]
                                kt_, vt_ = ["cak"], ["cav"]
                            else:
                                kap = cbk[hb, l, cidx, cc * 128:(cc + 1) * 128]
                                vap = cbv[:, l, cc, (vc - 128) // 64, :]
                                kt_, vt_ = ["cbk"], ["cbv"]
                            srcs.append((kap, kt_, vap, vt_, None))
                        if isA:
                            ms = range(max(0, 4 * hf - 1), min(8, 4 * hf + 5))
                        else:
                            ms = range(8)
                        for m in ms:
                            mask = maskb[:, m - 4 * hf + 1, :] if isA else None
                            srcs.append((zq[ki][hb, m * 128:(m + 1) * 128], zqall(ki),
                                         vtok[:, m, vc:vc + 64], ["vtok%d" % m], mask))
                        attn_run(qi, hh, None, srcs, qcols, oi, sink_col)

        stage("attn_done")
        kb.barrier()
        kb.off = W0
        NSEG = 128
        Ysum = A([128, 16, 3, 64], F32)
        bsT = A([128, 3, 1024], BF16)
        epi_off = kb.off
        AT = [[A([128, NSEG], BF16) for _ in range(3)] for _ in range(2)]
        BT = [[A([128, NSEG], BF16) for _ in range(3)] for _ in range(2)]
        KT = [[A([128, NSEG], BF16) for _ in range(3)] for _ in range(2)]
        RT = [[A([128, NSEG], BF16) for _ in range(3)] for _ in range(2)]
        RK = [[A([128, NSEG], BF16) for _ in range(3)] for _ in range(2)]
        eL = [[A([128, NSEG], F32) for _ in range(3)] for _ in range(2)]
        tq = [A([128, NSEG], F32) for _ in range(8)]
        t1b = A([128, NSEG], BF16)
        srcd = [[A([128, 128], BF16) for _ in range(4)] for _ in range(2)]
        Gka = [A([128, 128], BF16) for _ in range(2)]
        RTd, TK3, Gk, AhTd, Vp = {}, {}, {}, {}, {}
        LVc = [A([128, 3, 512], F32) for _ in range(2)]
        for pr in range(3):
            for c in range(2):
                RTd[(pr, c)] = A([128, 128], BF16)
                TK3[(pr, c)] = A([128, 2, 128], BF16)
                Vp[(pr, c)] = A([128, 64], BF16)
                Gk[(pr, c)] = A([128, 256], BF16)
                AhTd[(pr, c)] = A([128, 128], BF16)
        S32 = A([128, 3, 64], F32)
        PCb = [A([128, 2, 3, 64], F32) for _ in range(2)]
        onesf = A([128, 64], F32)
        kb.memset("dve", onesf[:], 1.0, w=["onesf"])
        Sb = A([128, 3, 64], BF16)
        stmp = A([128, 192], F32)
        Ut = A([128, 3, 64], BF16)
        CR, CK, CV, CWA, CG = zc[0:3], zc[3:6], zc[6:9], zc[9], zc[10]
        zctok = lambda i: ["zc%d_0" % i, "zc%d_1" % i]
        kb.actf(CWA[0:64, :], CWA[0:64, :], AF.Tanh, r=zctok(9), w=zctok(9))
        kb.actf(CG[:, :], CG[:, :], AF.Sigmoid, r=zctok(10), w=zctok(10))
        bd3 = onesbd[:, :].rearrange("p (a b) -> p a b", b=64)
        ucount = [0]

        def bc2(ap):
            a = [list(x) for x in ap.ap]
            assert len(a) == 2, a
            return bass.AP(tensor=ap.tensor, offset=ap.offset, ap=[a[0], [0, 2], a[1]])

        def prep_piece(d, sgi, ps, pr):
            cols = slice(sgi * NSEG, (sgi + 1) * NSEG)
            if True:
                tqn = ["tq%d" % i for i in range(8)]
                kb.ts("dve", tq[0][:], CK[pr][:, cols], pv[:, b + 100 + pr:b + 101 + pr], None, ALU.mult,
                      r=zctok(3 + pr) + ["pv"], w=[tqn[0]])
                kb.actf(t1b[:], tq[0][:], AF.Square, r=[tqn[0]], w=["t1b"])
                kb.mm(PB[6][:, 0:NSEG], lhsT=onesbd[:], rhs=t1b[:], r=["t1b", "onesbd"], w=["pb6"])
                kb.actf(tq[2][:], PB[6][:, 0:NSEG], AF.Ln, bias=1e-12, scale=1.0, r=["pb6"], w=[tqn[2]])
                kb.actf(tq[2][:], tq[2][:], AF.Exp, scale=-0.5, r=[tqn[2]], w=[tqn[2]])
                kb.tt("dve", tq[0][:], tq[0][:], tq[2][:], ALU.mult, r=[tqn[0], tqn[2]], w=[tqn[0]])
                kb.mm(PB[6][:, 0:NSEG], lhsT=wlr[0:64, l, d, pr * 128:(pr + 1) * 128], rhs=CWA[0:64, cols],
                      r=["wlr"] + zctok(9), w=["pb6"])
                kb.actf(tq[3][:], PB[6][:, 0:NSEG], AF.Sigmoid, bias=pv[:, b + 88 + 3 * d + pr:b + 89 + 3 * d + pr],
                        scale=1.0, r=["pb6", "pv"], w=[tqn[3]])
                kb.ts("dve", tq[3][:], tq[3][:], -0.6065306597126334, None, ALU.mult, r=[tqn[3]], w=[tqn[3]])
                if d == 0:
                    kb.op("dve", lambda e, o=tq[4], m=scanm, x=tq[3]: e.tensor_tensor_scan(
                        out=o[:, :], data0=m[:, 0, :], data1=x[:, :], initial=0.0, op0=ALU.mult, op1=ALU.add),
                        r=[tqn[3], "scanm"], w=[tqn[4]])
                else:
                    kb.op("dve", lambda e, o=tq[4], m=scanm, x=tq[3]: e.tensor_tensor_scan(
                        out=o[:, ::-1], data0=m[:, 1, ::-1], data1=x[:, ::-1], initial=0.0, op0=ALU.mult, op1=ALU.add),
                        r=[tqn[3], "scanm"], w=[tqn[4]])
                kb.actf(eL[ps][pr][:], tq[4][:], AF.Exp, r=[tqn[4]], w=["eL%d_%d" % (pr, ps)])
                kb.tt("dve", tq[5][:], tq[4][:], tq[3][:], ALU.subtract, r=[tqn[4], tqn[3]], w=[tqn[5]])
                kb.actf(tq[5][:], tq[5][:], AF.Exp, r=[tqn[5]], w=[tqn[5]])
                kb.actf(tq[4][:], tq[4][:], AF.Exp, scale=-1.0, r=[tqn[4]], w=[tqn[4]])
                kb.mm(PB[6][:, 0:NSEG], lhsT=wlr[64:128, l, d, pr * 128:(pr + 1) * 128], rhs=CWA[64:128, cols],
                      r=["wlr"] + zctok(9), w=["pb6"])
                kb.actf(tq[6][:], PB[6][:, 0:NSEG], AF.Sigmoid, bias=pv[:, b + 94 + 3 * d + pr:b + 95 + 3 * d + pr],
                        scale=1.0, r=["pb6", "pv"], w=[tqn[6]])
                kb.ts("dve", tq[7][:], tq[6][:], pv[:, b + 103 + pr:b + 104 + pr], omka[:, l * 3 + pr:l * 3 + pr + 1],
                      ALU.mult, ALU.add, r=[tqn[6], "pv", "omka"], w=[tqn[7]])
                kb.tt("dve", tq[7][:], CK[pr][:, cols], tq[7][:], ALU.mult, r=zctok(3 + pr) + [tqn[7]], w=[tqn[7]])
                kb.stt(AT[ps][pr][:], tq[0][:], -1.0, tq[5][:], ALU.mult, ALU.mult, r=[tqn[0], tqn[5]], w=["AT%d_%d" % (pr, ps)])
                kb.tt("pool", RT[ps][pr][:], CR[pr][:, cols], eL[ps][pr][:], ALU.mult, r=zctok(pr) + ["eL%d_%d" % (pr, ps)], w=["RT%d_%d" % (pr, ps)])
                kb.tt("dve", tq[2][:], tq[0][:], tq[6][:], ALU.mult, r=[tqn[0], tqn[6]], w=[tqn[2]])
                kb.tt("dve", BT[ps][pr][:], tq[2][:], tq[4][:], ALU.mult, r=[tqn[2], tqn[4]], w=["BT%d_%d" % (pr, ps)])
                kb.tt("pool", KT[ps][pr][:], tq[7][:], tq[4][:], ALU.mult, r=[tqn[7], tqn[4]], w=["KT%d_%d" % (pr, ps)])
                kb.stt(RK[ps][pr][:], CR[pr][:, cols], pv[:, b + 106 + pr:b + 107 + pr], tq[7][:], ALU.mult, ALU.mult,
                       r=zctok(pr) + ["pv", tqn[7]], w=["RK%d_%d" % (pr, ps)])
                for c_ in range(2):
                    pc_ = c_ * 64 + (63 if d == 0 else 0)
                    kb.actf(PCb[ps][:, c_, pr, :], onesf[:], AF.Copy, scale=eL[ps][pr][:, pc_:pc_ + 1],
                            r=["onesf", "eL%d_%d" % (pr, ps)], w=["PCb%d_%d" % (ps, c_)])
                kb.mm(PB[6][:, 0:NSEG], lhsT=onesbd[:], rhs=RK[ps][pr][:], r=["RK%d_%d" % (pr, ps), "onesbd"], w=["pb6"])
                bt_ = "bsT%d_%d" % (pr, sgi)
                if d == 0:
                    kb.cp("act", bsT[:, pr, cols], PB[6][:, 0:NSEG], r=["pb6"], w=[bt_])
                else:
                    kb.tt("dve", bsT[:, pr, cols], bsT[:, pr, cols], PB[6][:, 0:NSEG], ALU.add, r=["pb6", bt_], w=[bt_])

        def seg(d, sgi, ps, nxt):
            cols = slice(sgi * NSEG, (sgi + 1) * NSEG)
            stage("rwkv_prep")
            units = [(pr, c) for c in range(2) for pr in range(3)]
            for uidx, (pr, c) in enumerate(units):
                u = (pr, c)
                ut = "un%d_%d" % (pr, c)
                cs = slice(c * 64, c * 64 + 64)
                gcols = slice(sgi * NSEG + c * 64, sgi * NSEG + c * 64 + 64)
                ucount[0] += 1
                si = ucount[0] % 2
                ATd, BTd, KTd, CVd = srcd[si]
                sn = ["srcd%d_%d" % (si, j) for j in range(4)]
                LVu = LVc[c][:, pr, :]
                lvt = "LV%d_%d" % (c, pr)
                as3 = lambda t: t[:, :].rearrange("p (a b) -> p a b", b=64)
                kb.tt("pool", as3(ATd), bc2(AT[ps][pr][:, cs]), bd3, ALU.mult, r=["AT%d_%d" % (pr, ps), "onesbd"], w=[sn[0]])
                kb.tt("pool", as3(BTd), bc2(BT[ps][pr][:, cs]), bd3, ALU.mult, r=["BT%d_%d" % (pr, ps), "onesbd"], w=[sn[1]])
                kb.tt("pool", as3(KTd), bc2(KT[ps][pr][:, cs]), bd3, ALU.mult, r=["KT%d_%d" % (pr, ps), "onesbd"], w=[sn[2]])
                kb.tt("pool", as3(CVd), bc2(CV[pr][:, gcols]), bd3, ALU.mult, r=zctok(6 + pr) + ["onesbd"], w=[sn[3]])
                kb.tt("pool", as3(RTd[u]), bc2(RT[ps][pr][:, cs]), bd3, ALU.mult, r=["RT%d_%d" % (pr, ps), "onesbd"], w=["RTd" + ut])
                kb.tr(PT[:, 0:128], ATd[:, :], identb[:], r=[sn[0], "identb"], w=["pt"])
                kb.tr(PT[:, 128:256], BTd[:, :], identb[:], r=[sn[1], "identb"], w=["pt"])
                kb.tr(PT[:, 256:384], KTd[:, :], identb[:], r=[sn[2], "identb"], w=["pt"])
                kb.tr(PT[:, 384:512], CVd[:, :], identb[:], r=[sn[3], "identb"], w=["pt"])
                for hh in range(2):
                    hb = slice(hh * 64, hh * 64 + 64)
                    kb.cp("act", LVu[hb, 384:448], PT[hb, hh * 64:hh * 64 + 64], r=["pt"], w=[lvt])
                    kb.cp("act", Vp[u][hb, :], PT[hb, 384 + hh * 64:384 + hh * 64 + 64], r=["pt"], w=["Vp" + ut])
                kb.cp("act", TK3[u][:, :, :], PT[:, 128:384].rearrange("p (a b) -> p a b", b=128), r=["pt"], w=["TK" + ut])
                bo = (uidx % 2) * 3
                gA, gAn = PB[bo], pbn[bo]
                gB, gBn = PB[bo + 1], pbn[bo + 1]
                kb.mm(gA[:, 0:128], lhsT=BTd[:, :], rhs=ATd[:, :], r=[sn[0], sn[1]], w=[gAn])
                kb.mm(gA[:, 128:256], lhsT=ATd[:, :], rhs=BTd[:, :], r=[sn[0], sn[1]], w=[gAn])
                kb.mm(gA[:, 256:384], lhsT=KTd[:, :], rhs=ATd[:, :], r=[sn[0], sn[2]], w=[gAn])
                kb.mm(gB[:, 0:128], lhsT=BTd[:, :], rhs=RTd[u][:, :], r=[sn[1], "RTd" + ut], w=[gBn])
                kb.mm(gB[:, 128:256], lhsT=KTd[:, :], rhs=RTd[u][:, :], r=[sn[2], "RTd" + ut], w=[gBn])
                kb.tt("dve", LVu[:, 0:256], gA[:, 0:256], rmask[:, d, 0:256], ALU.mult, r=[gAn, "rmask"], w=[lvt])
                kb.tt("dve", Gka[si][:, :], gA[:, 256:384], rmask[:, d, 256:384], ALU.mult, r=[gAn, "rmask"], w=["Gka%d" % si])
                kb.tt("dve", Gk[u][:, :], gB[:, 0:256], rmask[:, d, 384:640], ALU.mult, r=[gBn, "rmask"], w=["Gk" + ut])
                Hp, Hpn = PB[bo + 2], pbn[bo + 2]
                kb.mm(Hp[:, 0:64], lhsT=Gka[si][:, :], rhs=Vp[u][:, :], r=["Gka%d" % si, "Vp" + ut], w=[Hpn])
                kb.cp("act", LVu[:, 320:384], Hp[:, 0:64], r=[Hpn], w=[lvt])
            stage("rwkv_units")
            for k in range(6):
                for ui, (pr, c) in enumerate(units):
                    LVu = LVc[c][:, pr, :]
                    lvt = "LV%d_%d" % (c, pr)
                    b1, b1n = PB[ui], pbn[ui]
                    Ad, Bd, Hh = LVu[:, 0:128], LVu[:, 128:256], LVu[:, 320:448]
                    kb.mm(b1[:, 0:128], lhsT=Ad, rhs=Hh, r=[lvt], w=[b1n])
                    if k < 5:
                        kb.mm(b1[:, 128:256], lhsT=Bd, rhs=Ad, r=[lvt], w=[b1n])
                        kb.mm(b1[:, 256:384], lhsT=Ad, rhs=Bd, r=[lvt], w=[b1n])
                    if nxt is not None:
                        kb.replay(nxt, 2)
                    kb.tt("dve", LVu[:, 320:448], b1[:, 0:128], LVu[:, 320:448], ALU.add, r=[b1n, lvt], w=[lvt])
                    if k < 5:
                        kb.cp("dve", LVu[:, 0:256], b1[:, 128:384], r=[b1n], w=[lvt])
                    if nxt is not None:
                        kb.replay(nxt, 2)
            if nxt is not None:
                kb.replay(nxt, 100000)
            for (pr, c) in units:
                ut = "un%d_%d" % (pr, c)
                LVu = LVc[c][:, pr, :]
                lvt = "LV%d_%d" % (c, pr)
                kb.tt("pool", LVu[:, 0:128].rearrange("p (a b) -> p a b", b=64), bc2(LVu[:, 384:448]), bd3, ALU.mult,
                      r=[lvt, "onesbd"], w=[lvt])
                kb.tr(PB[6][:, 0:128], LVu[:, 0:128], identf[:], r=[lvt, "identf"], w=["pb6"])
                kb.cp("act", AhTd[(pr, c)][:, :], PB[6][:, 0:128], r=["pb6"], w=["AhT" + ut])
            stage("rwkv_levels")
            for c in ((0, 1) if d == 0 else (1, 0)):
                cg = sgi * 2 + c
                Up, Upn = PB[0], "pb0"
                YA, YAn = PB[1], "pb1"
                YB, YBn = PB[2], "pb2"
                Sp, Spn = PB[3], "pb3"
                for pr in range(3):
                    kb.mm(Up[:, pr * 64:(pr + 1) * 64], lhsT=AhTd[(pr, c)][:, :], rhs=Sb[:, pr, :],
                          r=["AhTun%d_%d" % (pr, c), "Sb"], w=[Upn])
                kb.tt("dve", Ut[:, :, :], Up[:, 0:192].rearrange("p (a b) -> p a b", b=64), LVc[c][:, :, 320:384], ALU.add,
                      r=[Upn] + ["LV%d_%d" % (c, pr) for pr in range(3)], w=["Ut"])
                for pr in range(3):
                    u = (pr, c)
                    ut = "un%d_%d" % (pr, c)
                    kb.mm(YA[:, pr * 64:(pr + 1) * 64], lhsT=RTd[u][:, :], rhs=Sb[:, pr, :], r=["RTd" + ut, "Sb"], w=[YAn])
                    kb.mm(YB[:, pr * 64:(pr + 1) * 64], lhsT=Gk[u][:, 0:128], rhs=Ut[:, pr, :], start=True, stop=False,
                          r=["Gk" + ut, "Ut"], w=[YBn])
                    kb.mm(YB[:, pr * 64:(pr + 1) * 64], lhsT=Gk[u][:, 128:256], rhs=Vp[u][:, :], start=False, stop=True,
                          r=["Gk" + ut, "Vp" + ut], w=[YBn])
                    kb.mm(Sp[:, pr * 64:(pr + 1) * 64], lhsT=TK3[u][:, 0, :], rhs=Ut[:, pr, :], start=True, stop=False,
                          r=["TK" + ut, "Ut"], w=[Spn])
                    kb.mm(Sp[:, pr * 64:(pr + 1) * 64], lhsT=TK3[u][:, 1, :], rhs=Vp[u][:, :], start=False, stop=True,
                          r=["TK" + ut, "Vp" + ut], w=[Spn])
                yt = "Ysum%d" % cg
                ya = YA[:, 0:192].rearrange("p (a b) -> p a b", b=64)
                yb = YB[:, 0:192].rearrange("p (a b) -> p a b", b=64)
                if d == 0:
                    kb.cp("act", Ysum[:, cg, :, :], ya, r=[YAn], w=[yt])
                else:
                    kb.tt("dve", Ysum[:, cg, :, :], Ysum[:, cg, :, :], ya, ALU.add, r=[YAn, yt], w=[yt])
                kb.tt("dve", Ysum[:, cg, :, :], Ysum[:, cg, :, :], yb, ALU.add, r=[YBn, yt], w=[yt])
                kb.tt("dve", stmp[:], S32[:, :, :].rearrange("p a b -> p (a b)"), Sp[:, 0:192], ALU.add,
                      r=[Spn, "S32"], w=["stmp"])
                pcb = PCb[ps][:, c, :, :].rearrange("p a b -> p (a b)")
                kb.tt("dve", Sb[:, :, :].rearrange("p a b -> p (a b)"), stmp[:], pcb, ALU.mult,
                      r=["stmp", "PCb%d_%d" % (ps, c)], w=["Sb"])
                kb.tt("pool", S32[:, :, :].rearrange("p a b -> p (a b)"), stmp[:], pcb, ALU.mult,
                      r=["stmp", "PCb%d_%d" % (ps, c)], w=["S32"])

        nlanes = 1 if roped else 4
        chain = []
        for d in range(2):
            for lane in range(nlanes):
                segs = list(range(8)) if roped else [2 * lane, 2 * lane + 1]
                if d == 1:
                    segs = segs[::-1]
                for j, sgi in enumerate(segs):
                    chain.append((d, lane, sgi, j == 0, j == len(segs) - 1))

        def pieces(i):
            d_, lane_, sgi_, _, _ = chain[i]
            kb.defer = []
            for pr_ in range(3):
                prep_piece(d_, sgi_, i % 2, pr_)
            lst = kb.defer
            kb.defer = None
            return lst

        p0 = pieces(0)
        kb.replay(p0, 100000)
        for i, (d, lane, sgi, first, last) in enumerate(chain):
            if first:
                kb.memset("dve", S32[:, :, :], 0.0, w=["S32"])
                if roped:
                    kb.dma(S32[:, :, :], st0_d[:, l, d, :, :], r=["S32"], w=["S32"])
                kb.cp("dve", Sb[:, :, :], S32[:, :, :], r=["S32"], w=["Sb"])
            nxt = pieces(i + 1) if i + 1 < len(chain) else None
            seg(d, sgi, i % 2, nxt)
            stage("rwkv_seg1")
            if last and not roped:
                kb.dma(ost_d[:, l, lane, d, :, :], S32[:, :, :], r=["S32"])

        stage("rwkv_all")
        kb.barrier()
        kb.off = epi_off
        gst = A([128, 16], F32)
        ytmp = A([128, 3, 64], F32)
        ynb = A([128, 3, 64], BF16)
        ynd = A([128, 3, 128], BF16)
        ynT = A([128, 3, 1024], BF16)
        et1 = [A([128, 512], F32) for _ in range(2)]
        et2 = [A([128, 512], F32) for _ in range(2)]
        for c in range(16):
            yt = "Ysum%d" % c
            yv = Ysum[:, c, :, :]
            kb.op("dve", lambda e, o=gst, i=yv: e.tensor_reduce(out=o[:, 0:3], in_=i, axis=AX.X, op=ALU.add), r=[yt], w=["gst"])
            kb.tt("pool", ytmp[:, :, :], yv, yv, ALU.mult, r=[yt], w=["ytmp"])
            kb.op("dve", lambda e, o=gst, i=ytmp: e.tensor_reduce(out=o[:, 3:6], in_=i[:, :, :], axis=AX.X, op=ALU.add),
                  r=["ytmp"], w=["gst2"])
            kb.ts("dve", gst[:, 0:3], gst[:, 0:3], 1.0 / 64, None, ALU.mult, r=["gst"], w=["gst"])
            kb.tt("dve", gst[:, 6:9], gst[:, 0:3], gst[:, 0:3], ALU.mult, r=["gst"], w=["gst3"])
            kb.stt(gst[:, 9:12], gst[:, 3:6], 1.0 / 64, gst[:, 6:9], ALU.mult, ALU.subtract, r=["gst2", "gst3"], w=["gst4"])
            kb.actf(gst[:, 9:12], gst[:, 9:12], AF.Sqrt, bias=64e-5, scale=1.0, r=["gst4"], w=["gst4"])
            kb.recip(gst[:, 12:15], gst[:, 9:12], r=["gst4"], w=["gst5"])
            for pr in range(3):
                kb.ts("dve", ynb[:, pr, :], Ysum[:, c, pr, :], gst[:, pr:pr + 1], gst[:, 12 + pr:13 + pr],
                      ALU.subtract, ALU.mult, r=[yt, "gst", "gst5"], w=["ynb"])
            for pr in range(3):
                kb.tt("pool", ynd[:, pr, :].rearrange("p (a b) -> p a b", b=64), bc2(ynb[:, pr, :]), bd3, ALU.mult,
                      r=["ynb", "onesbd"], w=["ynd"])
            for pr in range(3):
                kb.tr(PT[:, pr * 128:(pr + 1) * 128], ynd[:, pr, :], identb[:], r=["ynd", "identb"], w=["pt"])
            for hh in range(2):
                hb = slice(hh * 64, hh * 64 + 64)
                src = PT[hb, 0:384].rearrange("p (a b) -> p a b", b=128)[:, :, hh * 64:hh * 64 + 64]
                kb.cp("act", ynT[hb, :, c * 64:(c + 1) * 64], src, r=["pt"], w=["ynT%d" % (c // 8)])
        for pr in range(3):
            for half in range(2):
                hs = HS[half]
                gp, gpn = nbank()
                kb.mm(gp[:], lhsT=wgup[:, l, pr * 128:(pr + 1) * 128], rhs=CG[:, hs], r=zctok(10) + ["wgup"], w=[gpn])
                kb.actf(et1[half][:], ynT[:, pr, hs], AF.Identity, bias=pv[:, b + 112 + pr:b + 113 + pr],
                        scale=pv[:, b + 109 + pr:b + 110 + pr], r=["ynT%d" % half, "pv"], w=["et1%d" % half])
                kb.tt("pool", et2[half][:], bsT[:, pr, hs], CV[pr][:, hs], ALU.mult,
                      r=["bsT%d_%d" % (pr, s_) for s_ in range(half * 4, half * 4 + 4)] + zctok(6 + pr), w=["et2%d" % half])
                kb.tt("pool", et1[half][:], et1[half][:], et2[half][:], ALU.add, r=["et1%d" % half, "et2%d" % half], w=["et1%d" % half])
                kb.tt("dve", oT[:, 5 + pr, hs], et1[half][:], gp[:], ALU.mult, r=["et1%d" % half, gpn], w=["o%d_%d" % (5 + pr, half)])
        stage("rwkv_epi")
        kb.barrier()
        kb.off = W0
        of32 = A([128, 8, 512], F32)
        tmpf = [A([128, 512], F32) for _ in range(2)]
        rs1 = A([128, 512], F32)
        sd = A([128, 512], F32)
        sqb = [A([128, 512], BF16) for _ in range(2)]

        def post_norm(half, gidx, of, pfx):
            hs = HS[half]
            pn, pnn = PB[4], "pb4"
            for k in range(8):
                sq = sqb[k % 2]
                kb.actf(sq[:], of[:, k, :], AF.Square, r=[pfx + str(k)], w=["sqb%d" % (k % 2)])
                kb.mm(pn[:], lhsT=ones128[:], rhs=sq[:], start=(k == 0), stop=(k == 7), r=["sqb%d" % (k % 2), "ones128"], w=[pnn])
            kb.actf(sd[:], pn[:], AF.Ln, bias=1e-6, scale=1.0 / 1024, r=[pnn], w=["sd"])
            kb.actf(rs1[:], sd[:], AF.Exp, scale=-0.5, r=["sd"], w=["rs1"])
            for k in range(8):
                t = tmpf[k % 2]
                kb.tt("pool", t[:], of[:, k, :], rs1[:], ALU.mult, r=[pfx + str(k), "rs1"], w=["tmpf%d" % (k % 2)])
                kb.stt(xT[:, k, hs], t[:], md[:, gidx, k:k + 1], xT[:, k, hs], ALU.mult, ALU.add,
                       r=["tmpf%d" % (k % 2), mdk, "x%d_%d" % (k, half)], w=["x%d_%d" % (k, half)])

        slabs = [slab(wout_d[l][:, s * 4096:(s + 1) * 4096], 4096) for s in range(2)]
        for half in range(2):
            for jo in range(8):
                sg_, tok = slabs[jo // 4]
                sv = sg_[:, :].rearrange("p (j k c) -> p j k c", k=8, c=128)
                bk_, bn = nbank()
                for k in range(8):
                    kb.mm(bk_[:], lhsT=sv[:, jo % 4, k, :], rhs=oT[:, k, HS[half]], start=(k == 0), stop=(k == 7),
                          r=[tok, "o%d_%d" % (k, half)], w=[bn])
                kb.cp("any", of32[:, jo, :], bk_[:], r=[bn], w=["of%d" % jo])
            post_norm(half, 2, of32, "of")

        stage("wout_done")
        kb.barrier()
        kb.off = zc_off
        ofA = A([128, 8, 512], F32)
        ofB = A([128, 8, 512], F32)
        kb.off = W0
        actT = A([128, 22, 1024], BF16)
        tmpf = [A([128, 512], F32) for _ in range(2)]
        rs1 = A([128, 512], F32)
        sd = A([128, 512], F32)
        sqb = [A([128, 512], BF16) for _ in range(2)]
        sil = [A([128, 512], F32) for _ in range(2)]
        h2T = hT
        for half in range(2):
            hs = HS[half]
            pn, pnn = PB[4], "pb4"
            for k in range(8):
                sq = sqb[k % 2]
                kb.actf(sq[:], xT[:, k, hs], AF.Square, r=["x%d_%d" % (k, half)], w=["sqb%d" % (k % 2)])
                kb.mm(pn[:], lhsT=ones128[:], rhs=sq[:], start=(k == 0), stop=(k == 7), r=["sqb%d" % (k % 2), "ones128"], w=[pnn])
            kb.actf(sd[:], pn[:], AF.Ln, bias=1e-6, scale=1.0 / 1024, r=[pnn], w=["sd"])
            kb.actf(rs1[:], sd[:], AF.Exp, scale=-0.5, r=["sd"], w=["rs1"])
            for k in range(8):
                t = tmpf[k % 2]
                kb.stt(t[:], xT[:, k, hs], md[:, 3, k:k + 1], rs1[:], ALU.mult, ALU.mult,
                       r=["x%d_%d" % (k, half), mdk, "rs1"], w=["tmpf%d" % (k % 2)])
                kb.actf(h2T[:, k, hs], t[:], AF.Identity, bias=md[:, 4, k:k + 1], scale=1.0,
                        r=["tmpf%d" % (k % 2), mdk], w=["h%d_%d" % (k, half)])
        for s in range(11):
            if ph == 0 and l == 0:
                mods_slab(1, s)
            sg_, tok = slab(wgu_d[l][:, s * 4096:(s + 1) * 4096], 4096)
            sv = sg_[:, :].rearrange("p (k c) -> p k c", c=512)
            for jj in range(2):
                j = 2 * s + jj
                for half in range(2):
                    bg, bgn = nbank()
                    bu, bun = nbank()
                    for k in range(8):
                        kb.mm(bg[:], lhsT=sv[:, k, jj * 256:jj * 256 + 128], rhs=h2T[:, k, HS[half]],
                              start=(k == 0), stop=(k == 7), r=[tok, "h%d_%d" % (k, half)], w=[bgn])
                    for k in range(8):
                        kb.mm(bu[:], lhsT=sv[:, k, jj * 256 + 128:jj * 256 + 256], rhs=h2T[:, k, HS[half]],
                              start=(k == 0), stop=(k == 7), r=[tok, "h%d_%d" % (k, half)], w=[bun])
                    kb.actf(sil[half][:], bg[:], AF.Silu, r=[bgn], w=["sil%d" % half])
                    kb.tt("dve", actT[:, j, HS[half]], sil[half][:], bu[:], ALU.mult, r=["sil%d" % half, bun],
                          w=["act%d_%d" % (j, half)])
        for jo in range(8):
            if ph == 0 and l == 0 and jo == 0:
                mods_slab(1, 11)
                mods_final(1)
            sg_, tok = slab(wdn_d[l][:, jo * 2816:(jo + 1) * 2816], 2816)
            sv = sg_[:, 0:2816].rearrange("p (k c) -> p k c", c=128)
            for half in range(2):
                bk_, bn = nbank()
                for k in range(22):
                    kb.mm(bk_[:], lhsT=sv[:, k, :], rhs=actT[:, k, HS[half]], start=(k == 0), stop=(k == 21),
                          r=[tok, "act%d_%d" % (k, half)], w=[bn])
                ofx, pfx = (ofA, "ofA") if half == 0 else (ofB, "ofB")
                kb.cp("any", ofx[:, jo, :], bk_[:], r=[bn], w=[pfx + str(jo)])
        post_norm(0, 5, ofA, "ofA")
        post_norm(1, 5, ofB, "ofB")

    allx = ["x%d_%d" % (k, hf) for k in range(8) for hf in range(2)]
    for ph in range(2):
        kb.barrier()
        kb.dma(xT[:], xin[ph], w=allx)
        for l in range(L):
            run_layer(ph, l)
            stage("layer_%d_%d" % (ph, l))
        kb.dma(yo[ph], xT[:], r=allx)
    kb.emit()
    return kb


_CACHE = {}


def kernel(**inputs):
    I = {k: np.asarray(v) for k, v in inputs.items()}
    if "kb" not in _CACHE:
        _CACHE["kb"] = build()
    kb = _CACHE["kb"]
    S = _prep_shared(I)
    in_maps = []
    for i in range(8):
        m = dict(S)
        m.update(_prep_core(I, i))
        in_maps.append(m)
    res = run_bass_kernel_spmd(kb.nc, in_maps, core_ids=list(range(8)))
    R = res.results
    f = np.float32
    y_prompt = np.empty((32, 256, 1024), f)
    y_sample = np.empty((2, 1024, 1024), f)
    nak = np.empty((32, L, 256, 2, 64), f)
    nav = np.empty((32, L, 256, 2, 64), f)
    nbk = np.empty((32, L, 256, 2, 64), f)
    nbv = np.empty((32, L, 256, 2, 64), f)
    nst = np.empty((32, L, 2, 6, 64, 64), f)
    for i in range(8):
        r = R[i]
        yp = r["yp"].transpose(1, 0, 2).reshape(1024, 1024).T
        y_prompt[4 * i:4 * i + 4] = yp.reshape(4, 256, 1024)
        if i < 2:
            y_sample[i] = r["ys"].transpose(1, 0, 2).reshape(1024, 1024).T
        oak = r["oak"]
        nak[4 * i:4 * i + 4] = oak.transpose(3, 1, 2, 0).reshape(4, 256, L, 2, 64).transpose(0, 2, 1, 3, 4)
        obk = r["obk"].reshape(2, 64, L, 1024)
        nbk[4 * i:4 * i + 4] = obk.transpose(3, 2, 0, 1).reshape(4, 256, L, 2, 64).transpose(0, 2, 1, 3, 4)
        oav = r["oav"]
        nav[4 * i:4 * i + 4] = oav.transpose(2, 0, 1, 3).reshape(4, 256, L, 2, 64).transpose(0, 2, 1, 3, 4)
        obv = r["obv"]
        nbv[4 * i:4 * i + 4] = obv.transpose(2, 0, 1, 3).reshape(4, 256, L, 2, 64).transpose(0, 2, 1, 3, 4)
        ost = r["ost"].reshape(2, 64, L, 4, 2, 3, 64)
        nst[4 * i:4 * i + 4] = ost.transpose(3, 2, 4, 5, 0, 6, 1).reshape(4, L, 2, 6, 64, 64)
    return (y_prompt, y_sample, nak, nav, nbk, nbv, nst)
```

```python
import numpy as np
import concourse.bass as bass
import concourse.mybir as mybir
from concourse.bass_utils import run_bass_kernel_spmd

F32 = mybir.dt.float32
BF16 = mybir.dt.bfloat16
AF = mybir.ActivationFunctionType
ALU = mybir.AluOpType
AX = mybir.AxisListType

L = 2
NDMASEM = 12
SB_LO = 16512
SB_HI = 229344
PVL = 128


class Op:
    __slots__ = ("eng", "fn", "deps", "is_dma", "need_inc", "sem", "val")

    def __init__(self, eng, fn, deps, is_dma):
        self.eng, self.fn, self.deps, self.is_dma = eng, fn, deps, is_dma
        self.need_inc = is_dma
        self.sem = None
        self.val = None


def _real(e):
    return "pool" if e == "poolq" else e


class KB:
    def __init__(self):
        self.nc = bass.Bass("TRN2", target_bir_lowering=False)
        nc = self.nc
        self.engs = {"pe": nc.tensor, "act": nc.scalar, "dve": nc.vector, "pool": nc.gpsimd, "sp": nc.sync}
        self.ops = []
        self.last_w = {}
        self.readers = {}
        self._ctx = []
        self.off = SB_LO
        self.last_eng = {}
        self.recent_dma = {"sp": [], "poolq": []}
        self.pending = {}
        self.alt = 0
        self.last_rg = {}
        self.defer = None
        self.nuniq = 0

    def dram(self, name, shape, dtype, kind):
        return self.nc.dram_tensor(name, list(shape), dtype, kind=kind).ap()

    def alloc(self, shape, dtype, name=None):
        nb = 4 if dtype == F32 else 2
        n = 1
        for s in shape[1:]:
            n *= s
        size = (n * nb + 31) // 32 * 32
        assert self.off + size <= SB_HI, ("SBUF overflow", name, self.off, size)
        self.nuniq += 1
        shp = list(shape)
        part = shp[0]
        shp[0] = 128
        t = self.nc.alloc_sbuf_tensor_at(name or ("t%d" % self.nuniq), shp, dtype, offset=self.off)
        self.off += size
        if part != 128:
            return t[0:part]
        return t

    def ps(self, name, shape, dtype=F32):
        cm = self.nc.psum_tensor(name, list(shape), dtype)
        t = cm.__enter__()
        self._ctx.append(cm)
        return t

    def replay(self, lst, n):
        cnt = 0
        while lst and cnt < n:
            if lst[0][0] == "pe" and cnt > 0 and n < 1000:
                break
            eng, fn, r, w, rgv = lst.pop(0)
            self.op(eng, fn, r, w, rgv=rgv)
            cnt += 1
            if eng == "pe" and n < 1000:
                break

    def op(self, eng, fn, r=(), w=(), extra=(), rgv=None):
        if self.defer is not None:
            self.defer.append((eng, fn, tuple(r), tuple(w), rgv))
            return None
        if rgv is not None:
            extra = []
            for t in w:
                prev = self.last_rg.get(t)
                if prev is not None and prev[0] != rgv:
                    extra.append(prev[1])
        o = self._op(eng, fn, r, w, extra)
        if rgv is not None:
            for t in w:
                self.last_rg[t] = (rgv, o)
        return o

    def _op(self, eng, fn, r=(), w=(), extra=()):
        is_dma = eng in ("sp", "poolq")
        deps = []
        for t in r:
            lw = self.last_w.get(t)
            if lw is not None:
                deps.append(lw)
            if t[:2] in ("pb", "pt"):
                for rd in self.readers.get(t, ()):
                    if rd.eng != eng:
                        deps.append(rd)
        for t in w:
            lw = self.last_w.get(t)
            if lw is not None:
                deps.append(lw)
            deps.extend(self.readers.get(t, ()))
        real = _real(eng)
        fdeps = []
        seen = set()
        for d in extra:
            if id(d) not in seen:
                seen.add(id(d))
                fdeps.append(d)
        for d in deps:
            if id(d) in seen:
                continue
            seen.add(id(d))
            if not d.is_dma and not is_dma and _real(d.eng) == real:
                if real == "pe":
                    continue
            fdeps.append(d)
        pb = self.pending.pop(eng, None)
        if pb:
            for d in pb:
                if id(d) not in seen and not (not d.is_dma and not is_dma and _real(d.eng) == real):
                    seen.add(id(d))
                    fdeps.append(d)
        o = Op(eng, fn, fdeps, is_dma)
        self.ops.append(o)
        for t in r:
            self.readers.setdefault(t, []).append(o)
        for t in w:
            self.last_w[t] = o
            self.readers[t] = []
        if is_dma:
            lst = self.recent_dma[eng]
            lst.append(o)
            if len(lst) > NDMASEM:
                lst.pop(0)
        else:
            self.last_eng[eng] = o
        return o

    def barrier(self):
        snap = list(self.last_eng.values()) + self.recent_dma["sp"] + self.recent_dma["poolq"]
        for e in ("pe", "act", "dve", "pool", "sp", "poolq"):
            self.pending[e] = list(snap) + self.pending.get(e, [])

    def mm(self, out, lhsT, rhs, start=True, stop=True, tp=None, r=(), w=()):
        rg = int(lhsT.start_partition())
        if tp is None:
            return self.op("pe", lambda e: e.matmul(out, lhsT=lhsT, rhs=rhs, start=start, stop=stop), r, w, rgv=rg)
        return self.op("pe", lambda e: e.matmul(out, lhsT=lhsT, rhs=rhs, start=start, stop=stop, tile_position=tp), r, w, rgv=rg)

    def tr(self, out, in_, ident, r=(), w=()):
        rg = int(in_.start_partition())
        return self.op("pe", lambda e: e.transpose(out, in_, ident), r, w, rgv=rg)

    def actf(self, out, in_, func, bias=None, scale=None, r=(), w=()):
        kw = {}
        if bias is not None:
            kw["bias"] = bias
        if scale is not None:
            kw["scale"] = scale
        return self.op("act", lambda e: e.activation(out=out, in_=in_, func=func, **kw), r, w)

    def tt(self, eng, out, in0, in1, op, r=(), w=()):
        return self.op(eng, lambda e: e.tensor_tensor(out=out, in0=in0, in1=in1, op=op), r, w)

    def ts(self, eng, out, in0, s1, s2, op0, op1=None, r=(), w=()):
        if op1 is None:
            return self.op(eng, lambda e: e.tensor_scalar(out=out, in0=in0, scalar1=s1, scalar2=None, op0=op0), r, w)
        return self.op(eng, lambda e: e.tensor_scalar(out=out, in0=in0, scalar1=s1, scalar2=s2, op0=op0, op1=op1), r, w)

    def stt(self, out, in0, scalar, in1, op0, op1, r=(), w=()):
        return self.op("dve", lambda e: e.scalar_tensor_tensor(out=out, in0=in0, scalar=scalar, in1=in1, op0=op0, op1=op1), r, w)

    def cp(self, eng, out, in_, r=(), w=()):
        if eng == "any":
            self.alt ^= 1
            eng = "act" if self.alt else "dve"
        if eng == "act":
            return self.op("act", lambda e: e.copy(out=out, in_=in_), r, w)
        return self.op(eng, lambda e: e.tensor_copy(out=out, in_=in_), r, w)

    def recip(self, out, in_, r=(), w=()):
        if FAST_RECIP:
            return self.op("dve", lambda e: e.reciprocal_approx_fast(out, in_), r, w)
        return self.op("dve", lambda e: e.reciprocal(out=out, in_=in_), r, w)

    def memset(self, eng, ap, val, r=(), w=()):
        return self.op(eng, lambda e: e.memset(ap, val), r, w)

    def dma(self, out, in_, r=(), w=(), q="sp"):
        return self.op(q, lambda e: e.dma_start(out=out, in_=in_), r, w)

    def emit(self):
        nc = self.nc
        for o in self.ops:
            for d in o.deps:
                d.need_inc = True
        sems = {}
        for e in ("pe", "act", "dve", "pool"):
            cm = nc.semaphore("s_" + e)
            sems[e] = cm.__enter__()
            self._ctx.append(cm)
        dsem = {}
        for q in ("sp", "poolq"):
            lst = []
            for i in range(NDMASEM):
                cm = nc.semaphore("d_%s_%d" % (q, i))
                lst.append(cm.__enter__())
                self._ctx.append(cm)
            dsem[q] = lst
        cnt = {e: 0 for e in sems}
        dcnt = {q: [0] * NDMASEM for q in dsem}
        dnext = {"sp": 0, "poolq": 0}
        for o in self.ops:
            if o.is_dma:
                k = dnext[o.eng]
                dnext[o.eng] = (k + 1) % NDMASEM
                o.sem = dsem[o.eng][k]
                o.val = dcnt[o.eng][k] + 16
                dcnt[o.eng][k] = o.val
            elif o.need_inc:
                cnt[o.eng] += 1
                o.sem = sems[o.eng]
                o.val = cnt[o.eng]
        waited = {}
        nwait = 0
        for o in self.ops:
            real = _real(o.eng)
            E = self.engs[real]
            if o.is_dma and o.val > 16:
                key = (real, id(o.sem))
                if waited.get(key, 0) < o.val - 16:
                    E.wait_ge(o.sem, o.val - 16)
                    waited[key] = o.val - 16
                    nwait += 1
            for d in o.deps:
                key = (real, id(d.sem))
                if waited.get(key, 0) < d.val:
                    E.wait_ge(d.sem, d.val)
                    waited[key] = d.val
                    nwait += 1
            ins = o.fn(E)
            if o.need_inc:
                ins.then_inc(o.sem, 16 if o.is_dma else 1)
        for q in ("sp", "poolq"):
            for k in range(NDMASEM):
                if dcnt[q][k] > 0:
                    self.engs[_real(q)].wait_ge(dsem[q][k], dcnt[q][k])
        self.stats = dict(n_ops=len(self.ops), n_wait=nwait)
        return nc


O_AQ, O_AK, O_AV, O_BQ, O_BK, O_BV, O_CR, O_CK, O_CV, O_CW, O_CA, O_CG = (
    0, 256, 384, 512, 896, 1024, 1152, 1536, 1920, 2304, 2368, 2432)


def _win_cols():
    def hd(base, h):
        return list(range(base + 64 * h, base + 64 * h + 64))

    def rot(c):
        return c[32:] + c[:32]

    pairs = [(hd(O_AQ, 0), hd(O_AQ, 1)), (hd(O_AQ, 2), hd(O_AQ, 3)),
             (hd(O_AK, 0), hd(O_AK, 0)), (hd(O_AK, 1), hd(O_AK, 1)),
             (hd(O_BQ, 0), hd(O_BQ, 1)), (hd(O_BQ, 2), hd(O_BQ, 3)), (hd(O_BQ, 4), hd(O_BQ, 5)),
             (hd(O_BK, 0), hd(O_BK, 0)), (hd(O_BK, 0), hd(O_BK, 1)), (hd(O_BK, 1), hd(O_BK, 1))]
    cols = []
    for a, b in pairs:
        cols += a + b
        cols += rot(a) + rot(b)
    cols += list(range(O_CR, O_CR + 384)) + list(range(O_CK, O_CK + 384)) + list(range(O_CV, O_CV + 384))
    cols += list(range(O_CW, O_CW + 128)) + list(range(O_CG, O_CG + 128))
    cols += list(range(O_AV, O_AV + 128)) + list(range(O_BV, O_BV + 128)) + list(range(O_CV, O_CV + 384))
    assert len(cols) == 36 * 128
    return np.array(cols)


def _fm(v):
    return np.ascontiguousarray(v.reshape(8, 128).T)


def _pair(v384):
    return np.ascontiguousarray(v384.reshape(3, 128).T)


def _prep_shared(I):
    f = np.float32
    S = {}
    wc = _win_cols()
    wmod = np.empty((L, 128, 12 * 4096), f)
    win = np.empty((L, 128, 9 * 4096), f)
    wout = np.empty((L, 128, 2 * 4096), f)
    wgu = np.empty((L, 128, 11 * 4096), f)
    wdn = np.empty((L, 128, 8 * 2816), f)
    for l in range(L):
        wmod[l] = I["w_mod"][l].reshape(8, 128, 12, 512).transpose(1, 2, 0, 3).reshape(128, -1)
        win[l] = I["w_in"][l][:, wc].reshape(8, 128, 9, 512).transpose(1, 2, 0, 3).reshape(128, -1)
        wout[l] = I["w_out"][l].reshape(8, 128, 2, 4, 128).transpose(1, 2, 3, 0, 4).reshape(128, -1)
        g = I["w_gu"][l][:, :2816].reshape(8, 128, 11, 2, 128)
        u = I["w_gu"][l][:, 2816:].reshape(8, 128, 11, 2, 128)
        gu = np.stack([g, u], axis=4)
        wgu[l] = gu.transpose(1, 2, 0, 3, 4, 5).reshape(128, -1)
        wdn[l] = I["w_down"][l].reshape(22, 128, 8, 128).transpose(1, 2, 0, 3).reshape(128, -1)
    S.update(wmod=wmod, win=win, wout=wout, wgu=wgu, wdn=wdn)
    pv = np.zeros((128, L * PVL), f)
    for l in range(L):
        b = l * PVL
        pv[:, b + 0:b + 8] = _fm(I["norm_mix_pre"][l])
        pv[:, b + 8:b + 16] = _fm(I["norm_mix_post"][l])
        pv[:, b + 16:b + 24] = _fm(I["norm_ffn_pre"][l])
        pv[:, b + 24:b + 32] = _fm(I["norm_ffn_post"][l])
        pv[:, b + 32:b + 80] = np.ascontiguousarray(I["b_mod"][l].reshape(48, 128).T)
        gq, gk = I["b_q_norm"][l], I["b_k_norm"][l]
        rot = lambda v: np.concatenate([v[32:], v[:32]])
        pv[:, b + 80] = np.tile(gq, 2)
        pv[:, b + 81] = np.tile(gk, 2)
        pv[:, b + 82] = np.tile(rot(gq), 2)
        pv[:, b + 83] = np.tile(rot(gk), 2)
        pv[:, b + 84:b + 88] = I["a_sink"][l][None, :]
        for d in range(2):
            pv[:, b + 88 + 3 * d:b + 91 + 3 * d] = _pair(I["c_w0"][l, d])
            pv[:, b + 94 + 3 * d:b + 97 + 3 * d] = _pair(I["c_a0"][l, d])
        pv[:, b + 100:b + 103] = _pair(I["c_k_k"][l])
        pv[:, b + 103:b + 106] = _pair(I["c_k_a"][l])
        pv[:, b + 106:b + 109] = _pair(I["c_r_k"][l].reshape(384))
        pv[:, b + 109:b + 112] = _pair(I["c_ln_w"][l])
        pv[:, b + 112:b + 115] = _pair(I["c_ln_b"][l])
    S["pv"] = pv
    wlr = np.empty((128, L, 2, 384), f)
    wg = np.empty((128, L, 384), f)
    for l in range(L):
        for d in range(2):
            wlr[0:64, l, d] = I["c_w_up"][l, d]
            wlr[64:128, l, d] = I["c_a_up"][l, d]
        wg[:, l] = I["c_g_up"][l]
    S["wlr"] = wlr
    S["wgup"] = wg
    T = 1024
    row = np.repeat(np.arange(T // 64), 64).astype(f)
    col = (np.arange(T) % 64).astype(f)
    fr = (np.float32(10000.0) ** (-np.arange(16, dtype=f) / np.float32(16))).astype(f)
    ang = np.concatenate([row[:, None] * fr, col[:, None] * fr], axis=-1).astype(f)
    cs, sn = np.cos(ang).astype(f), np.sin(ang).astype(f)
    C64 = np.concatenate([cs, cs], axis=1).T
    S64 = np.concatenate([-sn, sn], axis=1).T
    rope = np.empty((128, 2, T), f)
    rope[:, 0] = np.tile(C64, (2, 1))
    rope[:, 1] = np.tile(S64, (2, 1))
    S["rope"] = rope
    bb = np.arange(128)[:, None]
    qq = np.arange(512)[None, :]
    mb = np.empty((128, 6, 512), f)
    for j in range(6):
        mb[:, j] = (np.abs((j - 1) * 128 + bb - qq) <= 128).astype(f)
    S["maskb"] = mb
    r_ = np.arange(64)[:, None]
    c_ = np.arange(64)[None, :]
    rm = np.zeros((128, 2, 640), f)
    for d in range(2):
        up = (c_ > r_) if d == 0 else (c_ < r_)
        upe = (c_ >= r_) if d == 0 else (c_ <= r_)
        lo = (c_ < r_) if d == 0 else (c_ > r_)
        for j, m in enumerate([up, lo, up, upe, upe]):
            rm[0:64, d, j * 128:j * 128 + 64] = m
            rm[64:128, d, j * 128 + 64:j * 128 + 128] = m
    S["rmask"] = rm
    S["ident"] = np.eye(128, dtype=f)
    sm = np.ones((128, 2, 128), f)
    sm[:, 0, 0::64] = 0.0
    sm[:, 1, 63::64] = 0.0
    S["scanm"] = sm
    ob = np.zeros((128, 128), f)
    ob[:64, :64] = 1.0
    ob[64:, 64:] = 1.0
    S["onesbd"] = ob
    return S


def _prep_core(I, i):
    f = np.float32
    C = {}
    xp = I["x_prompt"][4 * i:4 * i + 4].reshape(1024, 1024)
    C["xp"] = np.ascontiguousarray(xp.T.reshape(8, 128, 1024).transpose(1, 0, 2))
    b = i % 2
    C["xs"] = np.ascontiguousarray(I["x_sample"][b].T.reshape(8, 128, 1024).transpose(1, 0, 2))
    cv = np.empty((128, 8, 2), f)
    cv[:, :, 0] = _fm(I["c_ctx"])
    cv[:, :, 1] = _fm(I["c"][b])
    C["cvec"] = cv
    ak = I["cache_a_k"][b]
    bk = I["cache_b_k"][b]
    cak = np.empty((128, L, 2, 256), f)
    cbk = np.empty((128, L, 3, 256), f)
    for l in range(L):
        for kv in range(2):
            kt = ak[l, :, kv, :].T
            cak[0:64, l, kv] = kt
            cak[64:128, l, kv] = kt
        k0, k1 = bk[l, :, 0, :].T, bk[l, :, 1, :].T
        cbk[0:64, l, 0], cbk[64:128, l, 0] = k0, k0
        cbk[0:64, l, 1], cbk[64:128, l, 1] = k0, k1
        cbk[0:64, l, 2], cbk[64:128, l, 2] = k1, k1
    C["cak"], C["cbk"] = cak, cbk
    C["cav"] = np.ascontiguousarray(I["cache_a_v"][b].reshape(L, 2, 128, 2, 64).transpose(2, 0, 1, 3, 4))
    C["cbv"] = np.ascontiguousarray(I["cache_b_v"][b].reshape(L, 2, 128, 2, 64).transpose(2, 0, 1, 3, 4))
    st = I["state_c"][b]
    C["st0"] = np.ascontiguousarray(st.reshape(L, 2, 3, 2, 64, 64).transpose(3, 5, 0, 1, 2, 4).reshape(128, L, 2, 3, 64))
    return C


_KBREF = [None]


class _Stop(Exception):
    pass


FAST_RECIP = False
STOP = [None]
DBG = set()
_seen_tags = []


def stage(tag):
    if tag not in _seen_tags:
        _seen_tags.append(tag)
        if STOP[0] == tag:
            raise _Stop()


def build():
    del _seen_tags[:]
    try:
        return _build()
    except _Stop:
        kb = _KBREF[0]
        kb.emit()
        return kb


def _build():
    kb = KB()
    _KBREF[0] = kb
    A = kb.alloc
    din = lambda n, s: kb.dram(n, s, F32, "ExternalInput")
    dout = lambda n, s: kb.dram(n, s, F32, "ExternalOutput")
    xin = [din("xp", [128, 8, 1024]), din("xs", [128, 8, 1024])]
    cvec_d = din("cvec", [128, 8, 2])
    pv_d = din("pv", [128, L * PVL])
    rope_d = din("rope", [128, 2, 1024])
    maskb_d = din("maskb", [128, 6, 512])
    rmask_d = din("rmask", [128, 2, 640])
    ident_d = din("ident", [128, 128])
    scanm_d = din("scanm", [128, 2, 128])
    onesbd_d = din("onesbd", [128, 128])
    cak_d = din("cak", [128, L, 2, 256])
    cbk_d = din("cbk", [128, L, 3, 256])
    cav_d = din("cav", [128, L, 2, 2, 64])
    cbv_d = din("cbv", [128, L, 2, 2, 64])
    st0_d = din("st0", [128, L, 2, 3, 64])
    wmod_d = din("wmod", [L, 128, 12 * 4096])
    win_d = din("win", [L, 128, 9 * 4096])
    wout_d = din("wout", [L, 128, 2 * 4096])
    wgu_d = din("wgu", [L, 128, 11 * 4096])
    wdn_d = din("wdn", [L, 128, 8 * 2816])
    wlr_d = din("wlr", [128, L, 2, 384])
    wgup_d = din("wgup", [128, L, 384])
    yo = [dout("yp", [128, 8, 1024]), dout("ys", [128, 8, 1024])]
    oak_d = dout("oak", [64, L, 2, 1024])
    obk_d = dout("obk", [128, L, 1024])
    oav_d = dout("oav", [128, L, 8, 128])
    obv_d = dout("obv", [128, L, 8, 128])
    ost_d = dout("ost", [128, L, 4, 2, 3, 64])

    xT = A([128, 8, 1024], F32, "xT")
    pv = A([128, L * PVL], F32, "pv")
    ropeT = A([128, 2, 1024], F32, "rope")
    maskb = A([128, 6, 512], BF16, "maskb")
    rmask = A([128, 2, 640], BF16, "rmask")
    identf = A([128, 128], F32, "identf")
    identb = A([128, 128], BF16, "identb")
    scanm = A([128, 2, 128], F32, "scanm")
    onesbd = A([128, 128], BF16, "onesbd")
    ones128 = A([128, 128], BF16, "ones128")
    ones64 = A([128, 64], BF16, "ones64")
    cvt = A([128, 8, 2], F32, "cvt")
    sct = A([128, 8, 2], BF16, "sct")
    modT = [A([128, 48, 2], F32, "modT%d" % l) for l in range(L)]
    MD = [[A([128, 6, 8], F32, "MD%d%d" % (l, ph)) for ph in range(2)] for l in range(L)]
    esink = A([128, L * 4], F32, "esink")
    omka = A([128, L * 3], F32, "omka")
    stg = [A([128, 4096], BF16, "stg%d" % i) for i in range(2)]
    hT = A([128, 8, 1024], BF16, "hT")
    oT = hT
    cak = A([128, L, 2, 256], BF16, "cak")
    cbk = A([128, L, 3, 256], BF16, "cbk")
    cav = A([128, L, 2, 2, 64], BF16, "cav")
    cbv = A([128, L, 2, 2, 64], BF16, "cbv")
    zc_off = kb.off
    zc = [A([128, 1024], BF16, "zc%d" % i) for i in range(11)]
    p2_off = kb.off
    vtok = A([128, 8, 256], BF16, "vtok")
    vtokc = A([64, 16, 384], BF16, "vtokc")
    wlr = A([128, L, 2, 384], BF16, "wlr")
    wgup = A([128, L, 384], BF16, "wgup")
    W0 = kb.off

    PB = [kb.ps("pb%d" % i, [128, 512], F32) for i in range(7)]
    PT = kb.ps("pt", [128, 1024], BF16)
    pbn = ["pb%d" % i for i in range(7)]

    kb.dma(pv[:], pv_d, w=["pv"])
    kb.dma(ropeT[:], rope_d, w=["rope"])
    kb.dma(maskb[:], maskb_d, w=["maskb"], q="poolq")
    kb.dma(rmask[:], rmask_d, w=["rmask"], q="poolq")
    kb.dma(identf[:], ident_d, w=["identf"])
    kb.dma(identb[:], ident_d, w=["identb"], q="poolq")
    kb.dma(scanm[:], scanm_d, w=["scanm"])
    kb.dma(onesbd[:], onesbd_d, w=["onesbd"], q="poolq")
    kb.dma(cvt[:], cvec_d, w=["cvt"])
    kb.dma(cak[:], cak_d, w=["cak"], q="poolq")
    kb.dma(cbk[:], cbk_d, w=["cbk"], q="poolq")
    kb.dma(cav[:], cav_d, w=["cav"], q="poolq")
    kb.dma(cbv[:], cbv_d, w=["cbv"], q="poolq")
    kb.dma(wlr[:], wlr_d, w=["wlr"], q="poolq")
    kb.dma(wgup[:], wgup_d, w=["wgup"], q="poolq")
    kb.memset("dve", ones128[:], 1.0, w=["ones128"])
    kb.memset("dve", ones64[:], 1.0, w=["ones64"])
    kb.actf(sct[:], cvt[:], AF.Silu, r=["cvt"], w=["sct"])
    for l in range(L):
        b = l * PVL
        kb.actf(esink[:, l * 4:l * 4 + 4], pv[:, b + 84:b + 88], AF.Exp, r=["pv"], w=["esink"])
        kb.ts("dve", omka[:, l * 3:l * 3 + 3], pv[:, b + 103:b + 106], -1.0, 1.0, ALU.mult, ALU.add, r=["pv"], w=["omka"])
    stage("consts")

    st_i = [0]

    def slab(src, n):
        i = st_i[0]
        st_i[0] ^= 1
        kb.dma(stg[i][:, 0:n], src, w=["stg%d" % i], q="poolq")
        return stg[i], "stg%d" % i

    bank_i = [0]

    def nbank():
        i = bank_i[0]
        bank_i[0] = (i + 1) % 4
        return PB[i], pbn[i]

    pmod = PB[5]

    def mods_slab(l, s):
        if True:
            sg_, tok = slab(wmod_d[l][:, s * 4096:(s + 1) * 4096], 4096)
            v = sg_[:, :].rearrange("p (k c) -> p k c", c=512)
            for q in range(4):
                mb_ = s * 4 + q
                for k in range(8):
                    kb.mm(pmod[:, mb_ * 2:mb_ * 2 + 2], lhsT=v[:, k, q * 128:(q + 1) * 128], rhs=sct[:, k, :],
                          start=(k == 0), stop=(k == 7), r=[tok, "sct"], w=["pb5"])

    def mods_final(l):
        pmv = pmod[:, 0:96].rearrange("p (m t) -> p m t", t=2)
        b = l * PVL
        for ph in range(2):
            kb.tt("dve", modT[l][:, :, ph], pmv[:, :, ph], pv[:, b + 32:b + 80], ALU.add, r=["pb5", "pv"], w=["modT%d" % l])
            m = MD[l][ph]
            mt = modT[l]
            tk = "MD%d%d" % (l, ph)
            kb.stt(m[:, 0, :], mt[:, 8:16, ph], 1.0, pv[:, b + 0:b + 8], ALU.add, ALU.mult, r=["modT%d" % l, "pv"], w=[tk])
            kb.cp("dve", m[:, 1, :], mt[:, 0:8, ph], r=["modT%d" % l], w=[tk])
            kb.tt("dve", m[:, 2, :], mt[:, 16:24, ph], pv[:, b + 8:b + 16], ALU.mult, r=["modT%d" % l, "pv"], w=[tk])
            kb.stt(m[:, 3, :], mt[:, 32:40, ph], 1.0, pv[:, b + 16:b + 24], ALU.add, ALU.mult, r=["modT%d" % l, "pv"], w=[tk])
            kb.cp("dve", m[:, 4, :], mt[:, 24:32, ph], r=["modT%d" % l], w=[tk])
            kb.tt("dve", m[:, 5, :], mt[:, 40:48, ph], pv[:, b + 24:b + 32], ALU.mult, r=["modT%d" % l, "pv"], w=[tk])


    for s_ in range(12):
        mods_slab(0, s_)
    mods_final(0)
    HS = [slice(0, 512), slice(512, 1024)]
    stage("mods")

    def run_layer(ph, l):
        b = l * PVL
        md = MD[l][ph]
        mdk = "MD%d%d" % (l, ph)
        roped = (ph == 1)
        kb.barrier()
        kb.off = W0
        tmpf = [A([128, 512], F32) for _ in range(2)]
        ropet = [A([128, 512], F32) for _ in range(2)]
        tmpb = [A([128, 512], F32) for _ in range(2)]
        rs = [A([128, 512], F32) for _ in range(2)]
        sd = A([128, 512], F32)
        sqb = [A([128, 512], BF16) for _ in range(2)]
        zq = [A([128, 1024], BF16) for _ in range(10)]
        pbuf = [A([128, 512], BF16) for _ in range(3)]
        dtmp = A([128, 512], F32)
        rD = A([128, 512], F32)
        akf = A([128, 512], F32)
        vf = A([128, 128], F32)
        uid = [0]

        def U():
            uid[0] += 1
            return "u%d_%d_%d" % (ph, l, uid[0])

        def norm_stats(srcs, src_toks, half, nfeat, ones_t, ones_tok, eps, rs_t, rs_tok):
            pn, pnn = PB[4], "pb4"
            n = len(srcs)
            for k in range(n):
                sq = sqb[k % 2]
                kb.actf(sq[:], srcs[k], AF.Square, r=[src_toks[k]], w=["sqb%d" % (k % 2)])
                kb.mm(pn[:], lhsT=ones_t, rhs=sq[:], start=(k == 0), stop=(k == n - 1),
                      r=["sqb%d" % (k % 2), ones_tok], w=[pnn])
            kb.actf(sd[:], pn[:], AF.Ln, bias=eps, scale=1.0 / nfeat, r=[pnn], w=["sd"])
            kb.actf(rs_t[:], sd[:], AF.Exp, scale=-0.5, r=["sd"], w=[rs_tok])

        for half in range(2):
            hs = HS[half]
            norm_stats([xT[:, k, hs] for k in range(8)], ["x%d_%d" % (k, half) for k in range(8)], half,
                       1024.0, ones128[:], "ones128", 1e-6, rs[0], "rs0")
            for k in range(8):
                t = tmpf[k % 2]
                kb.stt(t[:], xT[:, k, hs], md[:, 0, k:k + 1], rs[0][:], ALU.mult, ALU.mult,
                       r=["x%d_%d" % (k, half), mdk, "rs0"], w=["tmpf%d" % (k % 2)])
                kb.actf(hT[:, k, hs], t[:], AF.Identity, bias=md[:, 1, k:k + 1], scale=1.0,
                        r=["tmpf%d" % (k % 2), mdk], w=["h%d_%d" % (k, half)])

        hall = ["h%d_%d" % (k, hf) for k in range(8) for hf in range(2)]
        stage("prenorm")

        def in_block(sv, tok, q, half):
            bk_, bn = nbank()
            for k in range(8):
                kb.mm(bk_[:], lhsT=sv[:, k, q * 128:(q + 1) * 128], rhs=hT[:, k, HS[half]],
                      start=(k == 0), stop=(k == 7), r=[tok, "h%d_%d" % (k, half)], w=[bn])
            return bk_, bn

        for s in range(9):
            stage("win_s%d" % s)
            sg_, tok = slab(win_d[l][:, s * 4096:(s + 1) * 4096], 4096)
            sv = sg_[:, :].rearrange("p (k c) -> p k c", c=512)
            for q in range(4):
                blk = 4 * s + q
                if blk < 20:
                    p, isrot = blk // 2, blk % 2
                    if isrot and not roped:
                        continue
                    isB = p >= 4
                    gcol = b + (80 if p < 7 else 81) + (2 if isrot else 0)
                    for half in range(2):
                        hs = HS[half]
                        bk_, bn = in_block(sv, tok, q, half)
                        zt = "zq%d_%d" % (p, half)
                        if not isB:
                            if not roped:
                                kb.cp("any", zq[p][:, hs], bk_[:], r=[bn], w=[zt])
                                if p in (2, 3) and "noakcp" not in DBG:
                                    kb.cp("dve" if "akdve" in DBG else "any", akf[0:64, :], bk_[0:64, :], r=[bn], w=["akf"])
                                    if "noakdma" not in DBG:
                                        kb.dma(oak_d[:, l, p - 2, hs], akf[0:64, :], r=["akf"])
                            elif not isrot:
                                kb.tt("dve", ropet[half][:], bk_[:], ropeT[:, 0, hs], ALU.mult, r=[bn, "rope"], w=["ropet%d" % half])
                            else:
                                kb.tt("dve", tmpf[half][:], bk_[:], ropeT[:, 1, hs], ALU.mult, r=[bn, "rope"], w=["tmpf%d" % half])
                                kb.tt("pool", zq[p][:, hs], tmpf[half][:], ropet[half][:], ALU.add,
                                      r=["tmpf%d" % half, "ropet%d" % half], w=[zt])
                        else:
                            if not isrot:
                                kb.cp("act", tmpb[half][:], bk_[:], r=[bn], w=["tmpb%d" % half])
                                norm_stats([bk_[:]], [bn], half, 64.0, onesbd[:], "onesbd", 1e-6, rs[half], "rsB%d" % half)
                                if not roped:
                                    if p == 8:
                                        kb.stt(akf[:], tmpb[half][:], pv[:, gcol:gcol + 1], rs[half][:], ALU.mult, ALU.mult,
                                               r=["tmpb%d" % half, "rsB%d" % half, "pv"], w=["akf"])
                                        kb.dma(obk_d[:, l, hs], akf[:], r=["akf"])
                                        kb.cp("any", zq[p][:, hs], akf[:], r=["akf"], w=[zt])
                                    else:
                                        kb.stt(zq[p][:, hs], tmpb[half][:], pv[:, gcol:gcol + 1], rs[half][:], ALU.mult, ALU.mult,
                                               r=["tmpb%d" % half, "rsB%d" % half, "pv"], w=[zt])
                                else:
                                    kb.stt(tmpb[half][:], tmpb[half][:], pv[:, gcol:gcol + 1], rs[half][:], ALU.mult, ALU.mult,
                                           r=["tmpb%d" % half, "rsB%d" % half, "pv"], w=["tmpb%d" % half])
                                    kb.tt("pool", ropet[half][:], tmpb[half][:], ropeT[:, 0, hs], ALU.mult,
                                          r=["tmpb%d" % half, "rope"], w=["ropet%d" % half])
                            else:
                                kb.stt(tmpf[half][:], bk_[:], pv[:, gcol:gcol + 1], rs[half][:], ALU.mult, ALU.mult,
                                       r=[bn, "rsB%d" % half, "pv"], w=["tmpf%d" % half])
                                kb.tt("pool", tmpf[half][:], tmpf[half][:], ropeT[:, 1, hs], ALU.mult,
                                      r=["tmpf%d" % half, "rope"], w=["tmpf%d" % half])
                                kb.tt("dve", zq[p][:, hs], tmpf[half][:], ropet[half][:], ALU.add,
                                      r=["tmpf%d" % half, "ropet%d" % half], w=[zt])
                elif blk < 31:
                    ci = blk - 20
                    for half in range(2):
                        bk_, bn = in_block(sv, tok, q, half)
                        kb.cp("any", zc[ci][:, HS[half]], bk_[:], r=[bn], w=["zc%d_%d" % (ci, half)])
                elif blk == 31:
                    for tb in range(8):
                        bk_, bn = nbank()
                        for k in range(8):
                            kb.mm(bk_[:, 0:128], lhsT=hT[:, k, tb * 128:(tb + 1) * 128], rhs=sv[:, k, 384:512],
                                  start=(k == 0), stop=(k == 7), r=[tok, "h%d_%d" % (k, tb // 4)], w=[bn])
                        if not roped:
                            kb.cp("any", vf[:], bk_[:, 0:128], r=[bn], w=["vf"])
                            kb.dma(oav_d[:, l, tb, :], vf[:], r=["vf"])
                        kb.cp("any", vtok[:, tb, 0:128], bk_[:, 0:128], r=[bn], w=["vtok%d" % tb])
                elif blk == 32:
                    for tb in range(8):
                        bk_, bn = nbank()
                        for k in range(8):
                            kb.mm(bk_[:, 0:128], lhsT=hT[:, k, tb * 128:(tb + 1) * 128], rhs=sv[:, k, 0:128],
                                  start=(k == 0), stop=(k == 7), r=[tok, "h%d_%d" % (k, tb // 4)], w=[bn])
                        if not roped:
                            kb.cp("any", vf[:], bk_[:, 0:128], r=[bn], w=["vf"])
                            kb.dma(obv_d[:, l, tb, :], vf[:], r=["vf"])
                        kb.cp("any", vtok[:, tb, 128:256], bk_[:, 0:128], r=[bn], w=["vtok%d" % tb])

        zqall = lambda p: ["zq%d_0" % p, "zq%d_1" % p]
        stage("win_done")

        def attn_run(qi, hh, kt_ap_fn, srcs, qcols, ochunk, sink_col):
            hb = slice(hh * 64, hh * 64 + 64)
            nq = qcols.stop - qcols.start
            qhalf_toks = zqall(qi)
            pO, pOn = PB[2], "pb2"
            pD, pDn = PB[3], "pb3"
            n = len(srcs)
            for i, (kap, ktoks, vap, vtoks, mask) in enumerate(srcs):
                bs_, bsn = (PB[0], "pb0") if i % 2 == 0 else (PB[1], "pb1")
                kb.mm(bs_[:, 0:nq], lhsT=kap, rhs=zq[qi][hb, qcols], r=list(ktoks) + qhalf_toks, w=[bsn])
                P = pbuf[i % 3]
                pt = "pbuf%d" % (i % 3)
                kb.actf(P[:, 0:nq], bs_[:, 0:nq], AF.Exp, scale=0.125, r=[bsn], w=[pt])
                if mask is not None:
                    kb.tt("pool", P[:, 0:nq], P[:, 0:nq], mask, ALU.mult, r=[pt, "maskb"], w=[pt])
                tp = (0, 64) if hh else None
                kb.mm(pO[hb, 0:nq], lhsT=vap, rhs=P[:, 0:nq], start=(i == 0), stop=(i == n - 1), tp=tp,
                      r=[pt] + list(vtoks), w=[pOn])
                kb.mm(pD[hb, 0:nq], lhsT=ones64[:], rhs=P[:, 0:nq], start=(i == 0), stop=(i == n - 1), tp=tp,
                      r=[pt, "ones64"], w=[pDn])
            if sink_col is not None:
                kb.actf(dtmp[hb, 0:nq], pD[hb, 0:nq], AF.Ln, bias=esink[hb, sink_col:sink_col + 1], scale=1.0,
                        r=[pDn, "esink"], w=["dtmp"])
            else:
                kb.actf(dtmp[hb, 0:nq], pD[hb, 0:nq], AF.Ln, r=[pDn], w=["dtmp"])
            kb.actf(rD[hb, 0:nq], dtmp[hb, 0:nq], AF.Exp, scale=-1.0, r=["dtmp"], w=["rD"])
            kb.tt("dve", oT[hb, ochunk, qcols], pO[hb, 0:nq], rD[hb, 0:nq], ALU.mult, r=[pOn, "rD"],
                  w=["o%d_%d" % (ochunk, qcols.start // 512)])

        qdefs = [(0, 2, True, (0, 0), 0), (1, 3, True, (64, 64), 1),
                 (4, 7, False, (128, 128), 0), (5, 8, False, (128, 192), 1), (6, 9, False, (192, 192), 2)]
        kb.barrier()
        for oi, (qi, ki, isA, vcols, cidx) in enumerate(qdefs):
            for hh in range(2):
                hb = slice(hh * 64, hh * 64 + 64)
                vc = vcols[hh]
                sink_col = (l * 4 + (qi * 2 + hh)) if isA else None
                if not roped:
                    for sq in range(4):
                        qcols = slice(sq * 256, sq * 256 + 256)
                        srcs = []
                        for j in range(2):
                            tb = 2 * sq + j
                            srcs.append((zq[ki][hb, tb * 128:(tb + 1) * 128], zqall(ki),
                                         vtok[:, tb, vc:vc + 64], ["vtok%d" % tb], None))
                        attn_run(qi, hh, None, srcs, qcols, oi, sink_col)
                else:
                    for hf in range(2):
                        qcols = HS[hf]
                        srcs = []
                        for cc in range(2):
                            if isA:
                                kap = cak[hb, l, cidx, cc * 128:(cc + 1) * 128]
                                vap = cav[:, l, cc, vc // 64, :]
                                kt_, vt_ = ["cak"], ["cav"]
                            else:
                                kap = cbk[hb, l, cidx, cc * 128:(cc + 1) * 128]
                                vap = cbv[:, l, cc, (vc - 128) // 64, :]
                                kt_, vt_ = ["cbk"], ["cbv"]
                            srcs.append((kap, kt_, vap, vt_, None))
                        if isA:
                            ms = range(max(0, 4 * hf - 1), min(8, 4 * hf + 5))
                        else:
                            ms = range(8)
                        for m in ms:
                            mask = maskb[:, m - 4 * hf + 1, :] if isA else None
                            srcs.append((zq[ki][hb, m * 128:(m + 1) * 128], zqall(ki),
                                         vtok[:, m, vc:vc + 64], ["vtok%d" % m], mask))
                        attn_run(qi, hh, None, srcs, qcols, oi, sink_col)

        stage("attn_done")
        kb.barrier()
        kb.off = W0
        NSEG = 128
        Ysum = A([128, 16, 3, 64], F32)
        bsT = A([128, 3, 1024], BF16)
        epi_off = kb.off
        AT = [[A([128, NSEG], BF16) for _ in range(3)] for _ in range(2)]
        BT = [[A([128, NSEG], BF16) for _ in range(3)] for _ in range(2)]
        KT = [[A([128, NSEG], BF16) for _ in range(3)] for _ in range(2)]
        RT = [[A([128, NSEG], BF16) for _ in range(3)] for _ in range(2)]
        RK = [[A([128, NSEG], BF16) for _ in range(3)] for _ in range(2)]
        eL = [[A([128, NSEG], F32) for _ in range(3)] for _ in range(2)]
        tq = [A([128, NSEG], F32) for _ in range(8)]
        t1b = A([128, NSEG], BF16)
        srcd = [[A([128, 128], BF16) for _ in range(4)] for _ in range(2)]
        Gka = [A([128, 128], BF16) for _ in range(2)]
        RTd, TK3, Gk, AhTd, Vp = {}, {}, {}, {}, {}
        LVc = [A([128, 3, 512], F32) for _ in range(2)]
        for pr in range(3):
            for c in range(2):
                RTd[(pr, c)] = A([128, 128], BF16)
                TK3[(pr, c)] = A([128, 2, 128], BF16)
                Vp[(pr, c)] = A([128, 64], BF16)
                Gk[(pr, c)] = A([128, 256], BF16)
                AhTd[(pr, c)] = A([128, 128], BF16)
        S32 = A([128, 3, 64], F32)
        PCb = [A([128, 2, 3, 64], F32) for _ in range(2)]
        onesf = A([128, 64], F32)
        kb.memset("dve", onesf[:], 1.0, w=["onesf"])
        Sb = A([128, 3, 64], BF16)
        stmp = A([128, 192], F32)
        Ut = A([128, 3, 64], BF16)
        CR, CK, CV, CWA, CG = zc[0:3], zc[3:6], zc[6:9], zc[9], zc[10]
        zctok = lambda i: ["zc%d_0" % i, "zc%d_1" % i]
        kb.actf(CWA[0:64, :], CWA[0:64, :], AF.Tanh, r=zctok(9), w=zctok(9))
        kb.actf(CG[:, :], CG[:, :], AF.Sigmoid, r=zctok(10), w=zctok(10))
        bd3 = onesbd[:, :].rearrange("p (a b) -> p a b", b=64)
        ucount = [0]

        def bc2(ap):
            a = [list(x) for x in ap.ap]
            assert len(a) == 2, a
            return bass.AP(tensor=ap.tensor, offset=ap.offset, ap=[a[0], [0, 2], a[1]])

        def prep_piece(d, sgi, ps, pr):
            cols = slice(sgi * NSEG, (sgi + 1) * NSEG)
            if True:
                tqn = ["tq%d" % i for i in range(8)]
                kb.ts("dve", tq[0][:], CK[pr][:, cols], pv[:, b + 100 + pr:b + 101 + pr], None, ALU.mult,
                      r=zctok(3 + pr) + ["pv"], w=[tqn[0]])
                kb.actf(t1b[:], tq[0][:], AF.Square, r=[tqn[0]], w=["t1b"])
                kb.mm(PB[6][:, 0:NSEG], lhsT=onesbd[:], rhs=t1b[:], r=["t1b", "onesbd"], w=["pb6"])
                kb.actf(tq[2][:], PB[6][:, 0:NSEG], AF.Ln, bias=1e-12, scale=1.0, r=["pb6"], w=[tqn[2]])
                kb.actf(tq[2][:], tq[2][:], AF.Exp, scale=-0.5, r=[tqn[2]], w=[tqn[2]])
                kb.tt("dve", tq[0][:], tq[0][:], tq[2][:], ALU.mult, r=[tqn[0], tqn[2]], w=[tqn[0]])
                kb.mm(PB[6][:, 0:NSEG], lhsT=wlr[0:64, l, d, pr * 128:(pr + 1) * 128], rhs=CWA[0:64, cols],
                      r=["wlr"] + zctok(9), w=["pb6"])
                kb.actf(tq[3][:], PB[6][:, 0:NSEG], AF.Sigmoid, bias=pv[:, b + 88 + 3 * d + pr:b + 89 + 3 * d + pr],
                        scale=1.0, r=["pb6", "pv"], w=[tqn[3]])
                kb.ts("dve", tq[3][:], tq[3][:], -0.6065306597126334, None, ALU.mult, r=[tqn[3]], w=[tqn[3]])
                if d == 0:
                    kb.op("dve", lambda e, o=tq[4], m=scanm, x=tq[3]: e.tensor_tensor_scan(
                        out=o[:, :], data0=m[:, 0, :], data1=x[:, :], initial=0.0, op0=ALU.mult, op1=ALU.add),
                        r=[tqn[3], "scanm"], w=[tqn[4]])
                else:
                    kb.op("dve", lambda e, o=tq[4], m=scanm, x=tq[3]: e.tensor_tensor_scan(
                        out=o[:, ::-1], data0=m[:, 1, ::-1], data1=x[:, ::-1], initial=0.0, op0=ALU.mult, op1=ALU.add),
                        r=[tqn[3], "scanm"], w=[tqn[4]])
                kb.actf(eL[ps][pr][:], tq[4][:], AF.Exp, r=[tqn[4]], w=["eL%d_%d" % (pr, ps)])
                kb.tt("dve", tq[5][:], tq[4][:], tq[3][:], ALU.subtract, r=[tqn[4], tqn[3]], w=[tqn[5]])
                kb.actf(tq[5][:], tq[5][:], AF.Exp, r=[tqn[5]], w=[tqn[5]])
                kb.actf(tq[4][:], tq[4][:], AF.Exp, scale=-1.0, r=[tqn[4]], w=[tqn[4]])
                kb.mm(PB[6][:, 0:NSEG], lhsT=wlr[64:128, l, d, pr * 128:(pr + 1) * 128], rhs=CWA[64:128, cols],
                      r=["wlr"] + zctok(9), w=["pb6"])
                kb.actf(tq[6][:], PB[6][:, 0:NSEG], AF.Sigmoid, bias=pv[:, b + 94 + 3 * d + pr:b + 95 + 3 * d + pr],
                        scale=1.0, r=["pb6", "pv"], w=[tqn[6]])
                kb.ts("dve", tq[7][:], tq[6][:], pv[:, b + 103 + pr:b + 104 + pr], omka[:, l * 3 + pr:l * 3 + pr + 1],
                      ALU.mult, ALU.add, r=[tqn[6], "pv", "omka"], w=[tqn[7]])
                kb.tt("dve", tq[7][:], CK[pr][:, cols], tq[7][:], ALU.mult, r=zctok(3 + pr) + [tqn[7]], w=[tqn[7]])
                kb.stt(AT[ps][pr][:], tq[0][:], -1.0, tq[5][:], ALU.mult, ALU.mult, r=[tqn[0], tqn[5]], w=["AT%d_%d" % (pr, ps)])
                kb.tt("pool", RT[ps][pr][:], CR[pr][:, cols], eL[ps][pr][:], ALU.mult, r=zctok(pr) + ["eL%d_%d" % (pr, ps)], w=["RT%d_%d" % (pr, ps)])
                kb.tt("dve", tq[2][:], tq[0][:], tq[6][:], ALU.mult, r=[tqn[0], tqn[6]], w=[tqn[2]])
                kb.tt("dve", BT[ps][pr][:], tq[2][:], tq[4][:], ALU.mult, r=[tqn[2], tqn[4]], w=["BT%d_%d" % (pr, ps)])
                kb.tt("pool", KT[ps][pr][:], tq[7][:], tq[4][:], ALU.mult, r=[tqn[7], tqn[4]], w=["KT%d_%d" % (pr, ps)])
                kb.stt(RK[ps][pr][:], CR[pr][:, cols], pv[:, b + 106 + pr:b + 107 + pr], tq[7][:], ALU.mult, ALU.mult,
                       r=zctok(pr) + ["pv", tqn[7]], w=["RK%d_%d" % (pr, ps)])
                for c_ in range(2):
                    pc_ = c_ * 64 + (63 if d == 0 else 0)
                    kb.actf(PCb[ps][:, c_, pr, :], onesf[:], AF.Copy, scale=eL[ps][pr][:, pc_:pc_ + 1],
                            r=["onesf", "eL%d_%d" % (pr, ps)], w=["PCb%d_%d" % (ps, c_)])
                kb.mm(PB[6][:, 0:NSEG], lhsT=onesbd[:], rhs=RK[ps][pr][:], r=["RK%d_%d" % (pr, ps), "onesbd"], w=["pb6"])
                bt_ = "bsT%d_%d" % (pr, sgi)
                if d == 0:
                    kb.cp("act", bsT[:, pr, cols], PB[6][:, 0:NSEG], r=["pb6"], w=[bt_])
                else:
                    kb.tt("dve", bsT[:, pr, cols], bsT[:, pr, cols], PB[6][:, 0:NSEG], ALU.add, r=["pb6", bt_], w=[bt_])

        def seg(d, sgi, ps, nxt):
            cols = slice(sgi * NSEG, (sgi + 1) * NSEG)
            stage("rwkv_prep")
            units = [(pr, c) for c in range(2) for pr in range(3)]
            for uidx, (pr, c) in enumerate(units):
                u = (pr, c)
                ut = "un%d_%d" % (pr, c)
                cs = slice(c * 64, c * 64 + 64)
                gcols = slice(sgi * NSEG + c * 64, sgi * NSEG + c * 64 + 64)
                ucount[0] += 1
                si = ucount[0] % 2
                ATd, BTd, KTd, CVd = srcd[si]
                sn = ["srcd%d_%d" % (si, j) for j in range(4)]
                LVu = LVc[c][:, pr, :]
                lvt = "LV%d_%d" % (c, pr)
                as3 = lambda t: t[:, :].rearrange("p (a b) -> p a b", b=64)
                kb.tt("pool", as3(ATd), bc2(AT[ps][pr][:, cs]), bd3, ALU.mult, r=["AT%d_%d" % (pr, ps), "onesbd"], w=[sn[0]])
                kb.tt("pool", as3(BTd), bc2(BT[ps][pr][:, cs]), bd3, ALU.mult, r=["BT%d_%d" % (pr, ps), "onesbd"], w=[sn[1]])
                kb.tt("pool", as3(KTd), bc2(KT[ps][pr][:, cs]), bd3, ALU.mult, r=["KT%d_%d" % (pr, ps), "onesbd"], w=[sn[2]])
                kb.tt("pool", as3(CVd), bc2(CV[pr][:, gcols]), bd3, ALU.mult, r=zctok(6 + pr) + ["onesbd"], w=[sn[3]])
                kb.tt("pool", as3(RTd[u]), bc2(RT[ps][pr][:, cs]), bd3, ALU.mult, r=["RT%d_%d" % (pr, ps), "onesbd"], w=["RTd" + ut])
                kb.tr(PT[:, 0:128], ATd[:, :], identb[:], r=[sn[0], "identb"], w=["pt"])
                kb.tr(PT[:, 128:256], BTd[:, :], identb[:], r=[sn[1], "identb"], w=["pt"])
                kb.tr(PT[:, 256:384], KTd[:, :], identb[:], r=[sn[2], "identb"], w=["pt"])
                kb.tr(PT[:, 384:512], CVd[:, :], identb[:], r=[sn[3], "identb"], w=["pt"])
                for hh in range(2):
                    hb = slice(hh * 64, hh * 64 + 64)
                    kb.cp("act", LVu[hb, 384:448], PT[hb, hh * 64:hh * 64 + 64], r=["pt"], w=[lvt])
                    kb.cp("act", Vp[u][hb, :], PT[hb, 384 + hh * 64:384 + hh * 64 + 64], r=["pt"], w=["Vp" + ut])
                kb.cp("act", TK3[u][:, :, :], PT[:, 128:384].rearrange("p (a b) -> p a b", b=128), r=["pt"], w=["TK" + ut])
                bo = (uidx % 2) * 3
                gA, gAn = PB[bo], pbn[bo]
                gB, gBn = PB[bo + 1], pbn[bo + 1]
                kb.mm(gA[:, 0:128], lhsT=BTd[:, :], rhs=ATd[:, :], r=[sn[0], sn[1]], w=[gAn])
                kb.mm(gA[:, 128:256], lhsT=ATd[:, :], rhs=BTd[:, :], r=[sn[0], sn[1]], w=[gAn])
                kb.mm(gA[:, 256:384], lhsT=KTd[:, :], rhs=ATd[:, :], r=[sn[0], sn[2]], w=[gAn])
                kb.mm(gB[:, 0:128], lhsT=BTd[:, :], rhs=RTd[u][:, :], r=[sn[1], "RTd" + ut], w=[gBn])
                kb.mm(gB[:, 128:256], lhsT=KTd[:, :], rhs=RTd[u][:, :], r=[sn[2], "RTd" + ut], w=[gBn])
                kb.tt("dve", LVu[:, 0:256], gA[:, 0:256], rmask[:, d, 0:256], ALU.mult, r=[gAn, "rmask"], w=[lvt])
                kb.tt("dve", Gka[si][:, :], gA[:, 256:384], rmask[:, d, 256:384], ALU.mult, r=[gAn, "rmask"], w=["Gka%d" % si])
                kb.tt("dve", Gk[u][:, :], gB[:, 0:256], rmask[:, d, 384:640], ALU.mult, r=[gBn, "rmask"], w=["Gk" + ut])
                Hp, Hpn = PB[bo + 2], pbn[bo + 2]
                kb.mm(Hp[:, 0:64], lhsT=Gka[si][:, :], rhs=Vp[u][:, :], r=["Gka%d" % si, "Vp" + ut], w=[Hpn])
                kb.cp("act", LVu[:, 320:384], Hp[:, 0:64], r=[Hpn], w=[lvt])
            stage("rwkv_units")
            for k in range(6):
                for ui, (pr, c) in enumerate(units):
                    LVu = LVc[c][:, pr, :]
                    lvt = "LV%d_%d" % (c, pr)
                    b1, b1n = PB[ui], pbn[ui]
                    Ad, Bd, Hh = LVu[:, 0:128], LVu[:, 128:256], LVu[:, 320:448]
                    kb.mm(b1[:, 0:128], lhsT=Ad, rhs=Hh, r=[lvt], w=[b1n])
                    if k < 5:
                        kb.mm(b1[:, 128:256], lhsT=Bd, rhs=Ad, r=[lvt], w=[b1n])
                        kb.mm(b1[:, 256:384], lhsT=Ad, rhs=Bd, r=[lvt], w=[b1n])
                    if nxt is not None:
                        kb.replay(nxt, 2)
                    kb.tt("dve", LVu[:, 320:448], b1[:, 0:128], LVu[:, 320:448], ALU.add, r=[b1n, lvt], w=[lvt])
                    if k < 5:
                        kb.cp("act", LVu[:, 0:256], b1[:, 128:384], r=[b1n], w=[lvt])
                    if nxt is not None:
                        kb.replay(nxt, 2)
            if nxt is not None:
                kb.replay(nxt, 100000)
            for (pr, c) in units:
                ut = "un%d_%d" % (pr, c)
                LVu = LVc[c][:, pr, :]
                lvt = "LV%d_%d" % (c, pr)
                kb.tt("pool", LVu[:, 0:128].rearrange("p (a b) -> p a b", b=64), bc2(LVu[:, 384:448]), bd3, ALU.mult,
                      r=[lvt, "onesbd"], w=[lvt])
                kb.tr(PB[6][:, 0:128], LVu[:, 0:128], identf[:], r=[lvt, "identf"], w=["pb6"])
                kb.cp("act", AhTd[(pr, c)][:, :], PB[6][:, 0:128], r=["pb6"], w=["AhT" + ut])
            stage("rwkv_levels")
            for c in ((0, 1) if d == 0 else (1, 0)):
                cg = sgi * 2 + c
                Up, Upn = PB[0], "pb0"
                YA, YAn = PB[1], "pb1"
                YB, YBn = PB[2], "pb2"
                Sp, Spn = PB[3], "pb3"
                for pr in range(3):
                    kb.mm(Up[:, pr * 64:(pr + 1) * 64], lhsT=AhTd[(pr, c)][:, :], rhs=Sb[:, pr, :],
                          r=["AhTun%d_%d" % (pr, c), "Sb"], w=[Upn])
                kb.tt("dve", Ut[:, :, :], Up[:, 0:192].rearrange("p (a b) -> p a b", b=64), LVc[c][:, :, 320:384], ALU.add,
                      r=[Upn] + ["LV%d_%d" % (c, pr) for pr in range(3)], w=["Ut"])
                for pr in range(3):
                    kb.mm(PB[1][:, pr * 64:(pr + 1) * 64], lhsT=RTd[(pr, c)][:, :], rhs=Sb[:, pr, :],
                          r=["RTdun%d_%d" % (pr, c), "Sb"], w=["pb1"])
                for pr in range(3):
                    u = (pr, c)
                    ut = "un%d_%d" % (pr, c)
                    kb.mm(YB[:, pr * 64:(pr + 1) * 64], lhsT=Gk[u][:, 0:128], rhs=Ut[:, pr, :], start=True, stop=False,
                          r=["Gk" + ut, "Ut"], w=[YBn])
                    kb.mm(YB[:, pr * 64:(pr + 1) * 64], lhsT=Gk[u][:, 128:256], rhs=Vp[u][:, :], start=False, stop=True,
                          r=["Gk" + ut, "Vp" + ut], w=[YBn])
                    kb.mm(Sp[:, pr * 64:(pr + 1) * 64], lhsT=TK3[u][:, 0, :], rhs=Ut[:, pr, :], start=True, stop=False,
                          r=["TK" + ut, "Ut"], w=[Spn])
                    kb.mm(Sp[:, pr * 64:(pr + 1) * 64], lhsT=TK3[u][:, 1, :], rhs=Vp[u][:, :], start=False, stop=True,
                          r=["TK" + ut, "Vp" + ut], w=[Spn])
                yt = "Ysum%d" % cg
                ya = YA[:, 0:192].rearrange("p (a b) -> p a b", b=64)
                yb = YB[:, 0:192].rearrange("p (a b) -> p a b", b=64)
                if d == 0:
                    kb.cp("act", Ysum[:, cg, :, :], ya, r=[YAn], w=[yt])
                else:
                    kb.tt("dve", Ysum[:, cg, :, :], Ysum[:, cg, :, :], ya, ALU.add, r=[YAn, yt], w=[yt])
                kb.tt("dve", Ysum[:, cg, :, :], Ysum[:, cg, :, :], yb, ALU.add, r=[YBn, yt], w=[yt])
                kb.tt("dve", stmp[:], S32[:, :, :].rearrange("p a b -> p (a b)"), Sp[:, 0:192], ALU.add,
                      r=[Spn, "S32"], w=["stmp"])
                pcb = PCb[ps][:, c, :, :].rearrange("p a b -> p (a b)")
                kb.tt("dve", Sb[:, :, :].rearrange("p a b -> p (a b)"), stmp[:], pcb, ALU.mult,
                      r=["stmp", "PCb%d_%d" % (ps, c)], w=["Sb"])
                kb.tt("pool", S32[:, :, :].rearrange("p a b -> p (a b)"), stmp[:], pcb, ALU.mult,
                      r=["stmp", "PCb%d_%d" % (ps, c)], w=["S32"])

        nlanes = 1 if roped else 4
        chain = []
        for d in range(2):
            for lane in range(nlanes):
                segs = list(range(8)) if roped else [2 * lane, 2 * lane + 1]
                if d == 1:
                    segs = segs[::-1]
                for j, sgi in enumerate(segs):
                    chain.append((d, lane, sgi, j == 0, j == len(segs) - 1))

        def pieces(i):
            d_, lane_, sgi_, _, _ = chain[i]
            kb.defer = []
            for pr_ in range(3):
                prep_piece(d_, sgi_, i % 2, pr_)
            lst = kb.defer
            kb.defer = None
            return lst

        p0 = pieces(0)
        kb.replay(p0, 100000)
        for i, (d, lane, sgi, first, last) in enumerate(chain):
            if first:
                kb.memset("dve", S32[:, :, :], 0.0, w=["S32"])
                if roped:
                    kb.dma(S32[:, :, :], st0_d[:, l, d, :, :], r=["S32"], w=["S32"])
                kb.cp("dve", Sb[:, :, :], S32[:, :, :], r=["S32"], w=["Sb"])
            nxt = pieces(i + 1) if i + 1 < len(chain) else None
            seg(d, sgi, i % 2, nxt)
            stage("rwkv_seg1")
            if last and not roped:
                kb.dma(ost_d[:, l, lane, d, :, :], S32[:, :, :], r=["S32"])

        stage("rwkv_all")
        kb.barrier()
        kb.off = epi_off
        gst = A([128, 16], F32)
        ytmp = A([128, 3, 64], F32)
        ynb = A([128, 3, 64], BF16)
        ynd = A([128, 3, 128], BF16)
        ynT = A([128, 3, 1024], BF16)
        et1 = [A([128, 512], F32) for _ in range(2)]
        et2 = [A([128, 512], F32) for _ in range(2)]
        for c in range(16):
            yt = "Ysum%d" % c
            yv = Ysum[:, c, :, :]
            kb.op("dve", lambda e, o=gst, i=yv: e.tensor_reduce(out=o[:, 0:3], in_=i, axis=AX.X, op=ALU.add), r=[yt], w=["gst"])
            kb.tt("pool", ytmp[:, :, :], yv, yv, ALU.mult, r=[yt], w=["ytmp"])
            kb.op("dve", lambda e, o=gst, i=ytmp: e.tensor_reduce(out=o[:, 3:6], in_=i[:, :, :], axis=AX.X, op=ALU.add),
                  r=["ytmp"], w=["gst2"])
            kb.ts("dve", gst[:, 0:3], gst[:, 0:3], 1.0 / 64, None, ALU.mult, r=["gst"], w=["gst"])
            kb.tt("dve", gst[:, 6:9], gst[:, 0:3], gst[:, 0:3], ALU.mult, r=["gst"], w=["gst3"])
            kb.stt(gst[:, 9:12], gst[:, 3:6], 1.0 / 64, gst[:, 6:9], ALU.mult, ALU.subtract, r=["gst2", "gst3"], w=["gst4"])
            kb.actf(gst[:, 9:12], gst[:, 9:12], AF.Sqrt, bias=64e-5, scale=1.0, r=["gst4"], w=["gst4"])
            kb.recip(gst[:, 12:15], gst[:, 9:12], r=["gst4"], w=["gst5"])
            for pr in range(3):
                kb.ts("dve", ynb[:, pr, :], Ysum[:, c, pr, :], gst[:, pr:pr + 1], gst[:, 12 + pr:13 + pr],
                      ALU.subtract, ALU.mult, r=[yt, "gst", "gst5"], w=["ynb"])
            for pr in range(3):
                kb.tt("pool", ynd[:, pr, :].rearrange("p (a b) -> p a b", b=64), bc2(ynb[:, pr, :]), bd3, ALU.mult,
                      r=["ynb", "onesbd"], w=["ynd"])
            for pr in range(3):
                kb.tr(PT[:, pr * 128:(pr + 1) * 128], ynd[:, pr, :], identb[:], r=["ynd", "identb"], w=["pt"])
            for hh in range(2):
                hb = slice(hh * 64, hh * 64 + 64)
                src = PT[hb, 0:384].rearrange("p (a b) -> p a b", b=128)[:, :, hh * 64:hh * 64 + 64]
                kb.cp("act", ynT[hb, :, c * 64:(c + 1) * 64], src, r=["pt"], w=["ynT%d" % (c // 8)])
        for pr in range(3):
            for half in range(2):
                hs = HS[half]
                gp, gpn = nbank()
                kb.mm(gp[:], lhsT=wgup[:, l, pr * 128:(pr + 1) * 128], rhs=CG[:, hs], r=zctok(10) + ["wgup"], w=[gpn])
                kb.actf(et1[half][:], ynT[:, pr, hs], AF.Identity, bias=pv[:, b + 112 + pr:b + 113 + pr],
                        scale=pv[:, b + 109 + pr:b + 110 + pr], r=["ynT%d" % half, "pv"], w=["et1%d" % half])
                kb.tt("pool", et2[half][:], bsT[:, pr, hs], CV[pr][:, hs], ALU.mult,
                      r=["bsT%d_%d" % (pr, s_) for s_ in range(half * 4, half * 4 + 4)] + zctok(6 + pr), w=["et2%d" % half])
                kb.tt("pool", et1[half][:], et1[half][:], et2[half][:], ALU.add, r=["et1%d" % half, "et2%d" % half], w=["et1%d" % half])
                kb.tt("dve", oT[:, 5 + pr, hs], et1[half][:], gp[:], ALU.mult, r=["et1%d" % half, gpn], w=["o%d_%d" % (5 + pr, half)])
        stage("rwkv_epi")
        kb.barrier()
        kb.off = W0
        of32 = A([128, 8, 512], F32)
        tmpf = [A([128, 512], F32) for _ in range(2)]
        rs1 = A([128, 512], F32)
        sd = A([128, 512], F32)
        sqb = [A([128, 512], BF16) for _ in range(2)]

        def post_norm(half, gidx, of, pfx):
            hs = HS[half]
            pn, pnn = PB[4], "pb4"
            for k in range(8):
                sq = sqb[k % 2]
                kb.actf(sq[:], of[:, k, :], AF.Square, r=[pfx + str(k)], w=["sqb%d" % (k % 2)])
                kb.mm(pn[:], lhsT=ones128[:], rhs=sq[:], start=(k == 0), stop=(k == 7), r=["sqb%d" % (k % 2), "ones128"], w=[pnn])
            kb.actf(sd[:], pn[:], AF.Ln, bias=1e-6, scale=1.0 / 1024, r=[pnn], w=["sd"])
            kb.actf(rs1[:], sd[:], AF.Exp, scale=-0.5, r=["sd"], w=["rs1"])
            for k in range(8):
                t = tmpf[k % 2]
                kb.tt("pool", t[:], of[:, k, :], rs1[:], ALU.mult, r=[pfx + str(k), "rs1"], w=["tmpf%d" % (k % 2)])
                kb.stt(xT[:, k, hs], t[:], md[:, gidx, k:k + 1], xT[:, k, hs], ALU.mult, ALU.add,
                       r=["tmpf%d" % (k % 2), mdk, "x%d_%d" % (k, half)], w=["x%d_%d" % (k, half)])

        slabs = [slab(wout_d[l][:, s * 4096:(s + 1) * 4096], 4096) for s in range(2)]
        for half in range(2):
            for jo in range(8):
                sg_, tok = slabs[jo // 4]
                sv = sg_[:, :].rearrange("p (j k c) -> p j k c", k=8, c=128)
                bk_, bn = nbank()
                for k in range(8):
                    kb.mm(bk_[:], lhsT=sv[:, jo % 4, k, :], rhs=oT[:, k, HS[half]], start=(k == 0), stop=(k == 7),
                          r=[tok, "o%d_%d" % (k, half)], w=[bn])
                kb.cp("any", of32[:, jo, :], bk_[:], r=[bn], w=["of%d" % jo])
            post_norm(half, 2, of32, "of")

        stage("wout_done")
        kb.barrier()
        kb.off = zc_off
        ofA = A([128, 8, 512], F32)
        ofB = A([128, 8, 512], F32)
        kb.off = W0
        actT = A([128, 22, 1024], BF16)
        tmpf = [A([128, 512], F32) for _ in range(2)]
        rs1 = A([128, 512], F32)
        sd = A([128, 512], F32)
        sqb = [A([128, 512], BF16) for _ in range(2)]
        sil = [A([128, 512], F32) for _ in range(2)]
        h2T = hT
        for half in range(2):
            hs = HS[half]
            pn, pnn = PB[4], "pb4"
            for k in range(8):
                sq = sqb[k % 2]
                kb.actf(sq[:], xT[:, k, hs], AF.Square, r=["x%d_%d" % (k, half)], w=["sqb%d" % (k % 2)])
                kb.mm(pn[:], lhsT=ones128[:], rhs=sq[:], start=(k == 0), stop=(k == 7), r=["sqb%d" % (k % 2), "ones128"], w=[pnn])
            kb.actf(sd[:], pn[:], AF.Ln, bias=1e-6, scale=1.0 / 1024, r=[pnn], w=["sd"])
            kb.actf(rs1[:], sd[:], AF.Exp, scale=-0.5, r=["sd"], w=["rs1"])
            for k in range(8):
                t = tmpf[k % 2]
                kb.stt(t[:], xT[:, k, hs], md[:, 3, k:k + 1], rs1[:], ALU.mult, ALU.mult,
                       r=["x%d_%d" % (k, half), mdk, "rs1"], w=["tmpf%d" % (k % 2)])
                kb.actf(h2T[:, k, hs], t[:], AF.Identity, bias=md[:, 4, k:k + 1], scale=1.0,
                        r=["tmpf%d" % (k % 2), mdk], w=["h%d_%d" % (k, half)])
        for s in range(11):
            if ph == 0 and l == 0:
                mods_slab(1, s)
            sg_, tok = slab(wgu_d[l][:, s * 4096:(s + 1) * 4096], 4096)
            sv = sg_[:, :].rearrange("p (k c) -> p k c", c=512)
            for jj in range(2):
                j = 2 * s + jj
                for half in range(2):
                    bg, bgn = nbank()
                    bu, bun = nbank()
                    for k in range(8):
                        kb.mm(bg[:], lhsT=sv[:, k, jj * 256:jj * 256 + 128], rhs=h2T[:, k, HS[half]],
                              start=(k == 0), stop=(k == 7), r=[tok, "h%d_%d" % (k, half)], w=[bgn])
                    for k in range(8):
                        kb.mm(bu[:], lhsT=sv[:, k, jj * 256 + 128:jj * 256 + 256], rhs=h2T[:, k, HS[half]],
                              start=(k == 0), stop=(k == 7), r=[tok, "h%d_%d" % (k, half)], w=[bun])
                    kb.actf(sil[half][:], bg[:], AF.Silu, r=[bgn], w=["sil%d" % half])
                    kb.tt("dve", actT[:, j, HS[half]], sil[half][:], bu[:], ALU.mult, r=["sil%d" % half, bun],
                          w=["act%d_%d" % (j, half)])
        for jo in range(8):
            if ph == 0 and l == 0 and jo == 0:
                mods_slab(1, 11)
                mods_final(1)
            sg_, tok = slab(wdn_d[l][:, jo * 2816:(jo + 1) * 2816], 2816)
            sv = sg_[:, 0:2816].rearrange("p (k c) -> p k c", c=128)
            for half in range(2):
                bk_, bn = nbank()
                for k in range(22):
                    kb.mm(bk_[:], lhsT=sv[:, k, :], rhs=actT[:, k, HS[half]], start=(k == 0), stop=(k == 21),
                          r=[tok, "act%d_%d" % (k, half)], w=[bn])
                ofx, pfx = (ofA, "ofA") if half == 0 else (ofB, "ofB")
                kb.cp("any", ofx[:, jo, :], bk_[:], r=[bn], w=[pfx + str(jo)])
        post_norm(0, 5, ofA, "ofA")
        post_norm(1, 5, ofB, "ofB")

    allx = ["x%d_%d" % (k, hf) for k in range(8) for hf in range(2)]
    for ph in range(2):
        kb.barrier()
        kb.dma(xT[:], xin[ph], w=allx)
        for l in range(L):
            run_layer(ph, l)
            stage("layer_%d_%d" % (ph, l))
        kb.dma(yo[ph], xT[:], r=allx)
    kb.emit()
    return kb


_CACHE = {}


def kernel(**inputs):
    I = {k: np.asarray(v) for k, v in inputs.items()}
    if "kb" not in _CACHE:
        _CACHE["kb"] = build()
    kb = _CACHE["kb"]
    S = _prep_shared(I)
    in_maps = []
    for i in range(8):
        m = dict(S)
        m.update(_prep_core(I, i))
        in_maps.append(m)
    res = run_bass_kernel_spmd(kb.nc, in_maps, core_ids=list(range(8)))
    R = res.results
    f = np.float32
    y_prompt = np.empty((32, 256, 1024), f)
    y_sample = np.empty((2, 1024, 1024), f)
    nak = np.empty((32, L, 256, 2, 64), f)
    nav = np.empty((32, L, 256, 2, 64), f)
    nbk = np.empty((32, L, 256, 2, 64), f)
    nbv = np.empty((32, L, 256, 2, 64), f)
    nst = np.empty((32, L, 2, 6, 64, 64), f)
    for i in range(8):
        r = R[i]
        yp = r["yp"].transpose(1, 0, 2).reshape(1024, 1024).T
        y_prompt[4 * i:4 * i + 4] = yp.reshape(4, 256, 1024)
        if i < 2:
            y_sample[i] = r["ys"].transpose(1, 0, 2).reshape(1024, 1024).T
        oak = r["oak"]
        nak[4 * i:4 * i + 4] = oak.transpose(3, 1, 2, 0).reshape(4, 256, L, 2, 64).transpose(0, 2, 1, 3, 4)
        obk = r["obk"].reshape(2, 64, L, 1024)
        nbk[4 * i:4 * i + 4] = obk.transpose(3, 2, 0, 1).reshape(4, 256, L, 2, 64).transpose(0, 2, 1, 3, 4)
        oav = r["oav"]
        nav[4 * i:4 * i + 4] = oav.transpose(2, 0, 1, 3).reshape(4, 256, L, 2, 64).transpose(0, 2, 1, 3, 4)
        obv = r["obv"]
        nbv[4 * i:4 * i + 4] = obv.transpose(2, 0, 1, 3).reshape(4, 256, L, 2, 64).transpose(0, 2, 1, 3, 4)
        ost = r["ost"].reshape(2, 64, L, 4, 2, 3, 64)
        nst[4 * i:4 * i + 4] = ost.transpose(3, 2, 4, 5, 0, 6, 1).reshape(4, L, 2, 6, 64, 64)
    return (y_prompt, y_sample, nak, nav, nbk, nbv, nst)
```
